# Optimizing a Trainium2 kernel written in Bass

```python
import math
import jax, jax.numpy as jnp
from jax import lax
import numpy as np

D_MODEL = 1024
BATCH = 8
SEQ = 4096
DEPTH = 2

HEAD_DIM = 64
N_HEADS_DIL = 8
N_HEADS_WIN = 8
N_KV_WIN = 2
DIL_PATTERNS = ((128, 1), (512, 4), (2048, 16))
WIN_SIDE = 128
WIN_BLOCK = 128
D_DIL = N_HEADS_DIL * HEAD_DIM
D_WIN = N_HEADS_WIN * HEAD_DIM
D_KV_WIN = N_KV_WIN * HEAD_DIM
MIX_WIDTH = D_DIL + D_WIN
IN_COLS = 3 * D_DIL + D_WIN + 2 * D_KV_WIN
D_FF = ((8 * D_MODEL + 3 * 256 - 1) // (3 * 256)) * 256
EPS = 1e-6
NEG = -1e30
N_ALIBI = N_HEADS_WIN + N_HEADS_DIL

kernel_name = "hybrid_dilated_window_sink_encoder"


def rmsnorm(x, g):
    xf = x.astype(jnp.float32)
    y = xf * lax.rsqrt(jnp.mean(xf * xf, axis=-1, keepdims=True) + EPS)
    return (y * g.astype(jnp.float32)).astype(x.dtype)


def alibi_slopes(n):
    return np.array([2.0 ** (-8.0 * (i + 1) / n) for i in range(n)], dtype=np.float32)


def band_partial(q, k, v, side, blk, slopes, dist_scale):
    B, L, Hq, Dh = q.shape
    Hk = k.shape[2]
    grp = Hq // Hk
    n_blk = -(-L // blk)
    Lp = n_blk * blk
    pad = Lp - L
    qp = jnp.pad(q, ((0, 0), (0, pad), (0, 0), (0, 0))).reshape(B, n_blk, blk, Hk, grp, Dh)
    kp = jnp.pad(k, ((0, 0), (blk, pad + blk), (0, 0), (0, 0))).reshape(B, n_blk + 2, blk, Hk, Dh)
    vp = jnp.pad(v, ((0, 0), (blk, pad + blk), (0, 0), (0, 0))).reshape(B, n_blk + 2, blk, Hk, Dh)
    kw = jnp.concatenate([kp[:, 0:n_blk], kp[:, 1:n_blk + 1], kp[:, 2:n_blk + 2]], axis=2)
    vw = jnp.concatenate([vp[:, 0:n_blk], vp[:, 1:n_blk + 1], vp[:, 2:n_blk + 2]], axis=2)
    i_idx = np.arange(blk)[:, None]
    j_idx = np.arange(3 * blk)[None, :]
    rel = j_idx - blk - i_idx
    kpos = np.arange(n_blk)[:, None] * blk - blk + np.arange(3 * blk)[None, :]
    valid = (np.abs(rel) <= side)[None] & ((kpos >= 0) & (kpos < L))[:, None, :]
    dist = jnp.asarray(np.abs(rel).astype(np.float32) * dist_scale)
    sl = slopes.astype(jnp.float32).reshape(Hk, grp)
    s = jnp.einsum('bnqhgd,bnkhd->bhgnqk', qp.astype(jnp.float32), kw.astype(jnp.float32)) * (HEAD_DIM ** -0.5)
    s = s - sl[:, :, None, None, None] * dist[None, None, None]
    s = jnp.where(jnp.asarray(valid)[None, None, None], s, NEG)
    m = jnp.max(s, axis=-1)
    p = jnp.exp(s - m[..., None])
    l = jnp.sum(p, axis=-1)
    o = jnp.einsum('bhgnqk,bnkhd->bnqhgd', p, vw.astype(jnp.float32))
    m = m.transpose(0, 3, 4, 1, 2).reshape(B, Lp, Hq)[:, :L]
    l = l.transpose(0, 3, 4, 1, 2).reshape(B, Lp, Hq)[:, :L]
    o = o.reshape(B, Lp, Hq, Dh)[:, :L]
    return m, l, o


def dilated_attention(q, k, v, slopes):
    B, S, H, Dh = q.shape
    ms, ls, os_ = [], [], []
    for window, dil in DIL_PATTERNS:
        side = window // (2 * dil)
        L = S // dil

        def to_res(t):
            return t.reshape(B, L, dil, H, Dh).transpose(0, 2, 1, 3, 4).reshape(B * dil, L, H, Dh)

        m, l, o = band_partial(to_res(q), to_res(k), to_res(v), side, side, slopes, float(dil))
        ms.append(m.reshape(B, dil, L, H).transpose(0, 2, 1, 3).reshape(B, S, H))
        ls.append(l.reshape(B, dil, L, H).transpose(0, 2, 1, 3).reshape(B, S, H))
        os_.append(o.reshape(B, dil, L, H, Dh).transpose(0, 2, 1, 3, 4).reshape(B, S, H, Dh))
    m_all = jnp.stack(ms)
    l_all = jnp.stack(ls)
    o_all = jnp.stack(os_)
    w = jnp.exp(m_all - jnp.max(m_all, axis=0, keepdims=True))
    num = jnp.sum(w[..., None] * o_all, axis=0)
    den = jnp.sum(w * l_all, axis=0)
    return num / den[..., None]


def window_gqa_sink(q, k, v, slopes, sink):
    m, l, o = band_partial(q, k, v, WIN_SIDE, WIN_BLOCK, slopes, 1.0)
    sk = sink.astype(jnp.float32)
    M = jnp.maximum(m, sk)
    a = jnp.exp(m - M)
    den = l * a + jnp.exp(sk - M)
    return o * (a / den)[..., None]


def setup_inputs(seed: int = 0) -> dict:
    key = jax.random.key(seed)
    ks = jax.random.split(key, 12)
    f32 = jnp.float32
    x = jax.random.normal(ks[0], (BATCH, SEQ, D_MODEL), f32)
    g_mix = 1.0 + 0.02 * jax.random.normal(ks[1], (DEPTH, D_MODEL), f32)
    w_in = jax.random.normal(ks[2], (DEPTH, D_MODEL, IN_COLS), f32) * D_MODEL ** -0.5
    g_out_dil = 1.0 + 0.02 * jax.random.normal(ks[3], (DEPTH, D_DIL), f32)
    g_out_win = 1.0 + 0.02 * jax.random.normal(ks[4], (DEPTH, D_WIN), f32)
    sink = 0.1 * jax.random.normal(ks[5], (DEPTH, N_HEADS_WIN), f32)
    w_out = jax.random.normal(ks[6], (DEPTH, MIX_WIDTH, D_MODEL), f32) * MIX_WIDTH ** -0.5
    g_ffn = 1.0 + 0.02 * jax.random.normal(ks[7], (DEPTH, D_MODEL), f32)
    w_gate = jax.random.normal(ks[8], (DEPTH, D_MODEL, D_FF), f32) * D_MODEL ** -0.5
    w_up = jax.random.normal(ks[9], (DEPTH, D_MODEL, D_FF), f32) * D_MODEL ** -0.5
    w_down = jax.random.normal(ks[10], (DEPTH, D_FF, D_MODEL), f32) * D_FF ** -0.5
    g_final = 1.0 + 0.02 * jax.random.normal(ks[11], (D_MODEL,), f32)
    return {"x": x, "g_mix": g_mix, "w_in": w_in, "g_out_dil": g_out_dil, "g_out_win": g_out_win,
            "sink": sink, "w_out": w_out, "g_ffn": g_ffn, "w_gate": w_gate, "w_up": w_up,
            "w_down": w_down, "g_final": g_final}


def reference(x, g_mix, w_in, g_out_dil, g_out_win, sink, w_out, g_ffn, w_gate, w_up, w_down, g_final):
    B, S, _ = x.shape
    slopes = jnp.asarray(alibi_slopes(N_ALIBI))
    slopes_win = slopes[:N_HEADS_WIN]
    slopes_dil = slopes[N_HEADS_WIN:]
    o1 = D_DIL
    o2 = 2 * D_DIL
    o3 = 3 * D_DIL
    o4 = o3 + D_WIN
    o5 = o4 + D_KV_WIN
    for i in range(DEPTH):
        h = rmsnorm(x, g_mix[i])
        proj = h @ w_in[i]
        qa = proj[..., :o1].reshape(B, S, N_HEADS_DIL, HEAD_DIM)
        ka = proj[..., o1:o2].reshape(B, S, N_HEADS_DIL, HEAD_DIM)
        va = proj[..., o2:o3].reshape(B, S, N_HEADS_DIL, HEAD_DIM)
        qb = proj[..., o3:o4].reshape(B, S, N_HEADS_WIN, HEAD_DIM)
        kb = proj[..., o4:o5].reshape(B, S, N_KV_WIN, HEAD_DIM)
        vb = proj[..., o5:].reshape(B, S, N_KV_WIN, HEAD_DIM)
        ya = dilated_attention(qa, ka, va, slopes_dil).reshape(B, S, D_DIL).astype(x.dtype)
        yb = window_gqa_sink(qb, kb, vb, slopes_win, sink[i]).reshape(B, S, D_WIN).astype(x.dtype)
        y = jnp.concatenate([rmsnorm(ya, g_out_dil[i]), rmsnorm(yb, g_out_win[i])], axis=-1)
        x = x + y @ w_out[i]
        h = rmsnorm(x, g_ffn[i])
        x = x + (jax.nn.silu(h @ w_gate[i]) * (h @ w_up[i])) @ w_down[i]
    return rmsnorm(x, g_final)
```

```python
import math
from contextlib import ExitStack

import numpy as np
import ml_dtypes

import concourse.bass as bass
import concourse.mybir as mybir
from concourse.bass_utils import run_bass_kernel_spmd

F32 = mybir.dt.float32
BF16 = mybir.dt.bfloat16
AF = mybir.ActivationFunctionType
ALU = mybir.AluOpType

S = 4096
D = 1024
DEPTH = 2
DFF = 2816
NF = DFF // 128
EPS = 1e-6
NCORES = 8
WIN_COLS = 2432
NQK = 14
VW = 650
BIG = 131072.0
DILS = (1, 4, 16)
SLOPES = np.array([2.0 ** (-8.0 * (i + 1) / 16) for i in range(16)], dtype=np.float32)
SL_WIN = SLOPES[:8]
SL_DIL = SLOPES[8:]

SAME_ENGINE_SYNC = True


class Buf:
    __slots__ = ("name", "writer", "readers")

    def __init__(self, name):
        self.name = name
        self.writer = None
        self.readers = []


class Op:
    __slots__ = ("eng", "fn", "deps", "sig", "count", "dma_key", "dma_count")

    def __init__(self, eng, fn, dma_key):
        self.eng = eng
        self.fn = fn
        self.deps = []
        self.sig = False
        self.count = 0
        self.dma_key = dma_key
        self.dma_count = 0


ENGS = ("pe", "act", "dve", "pool", "sp")


class Sched:
    def __init__(self):
        self.ops = {e: [] for e in ENGS}
        self.dma_counts = {}
        self.all_bufs = []

    def buf(self, name):
        b = Buf(name)
        self.all_bufs.append(b)
        return b

    def op(self, eng, fn, reads=(), writes=(), dma_key=None, extra_deps=()):
        o = Op(eng, fn, dma_key)
        deps = []
        seen = set()

        def add(d):
            if d is not None and id(d) not in seen:
                seen.add(id(d))
                deps.append(d)

        for b in reads:
            add(b.writer)
        for b in writes:
            w = b.writer
            if not (w is not None and dma_key is not None and w.dma_key == dma_key and w.eng == eng):
                add(w)
            for r in b.readers:
                add(r)
        for d in extra_deps:
            add(d)
        for b in reads:
            b.readers.append(o)
        for b in writes:
            b.writer = o
            b.readers = []
        o.deps = deps
        for d in deps:
            d.sig = True
        if dma_key is not None:
            n = self.dma_counts.get(dma_key, 0) + 16
            self.dma_counts[dma_key] = n
            o.dma_count = n
        self.ops[eng].append(o)
        return o

    def barrier(self):
        lasts = []
        for e in ENGS:
            if self.ops[e]:
                lasts.append(self.ops[e][-1])
        last_dma = {}
        for e in ENGS:
            for o in self.ops[e]:
                if o.dma_key is not None:
                    last_dma[o.dma_key] = o
        deps = lasts + list(last_dma.values())
        bar = []
        for e in ENGS:
            bar.append(self.op(e, None, extra_deps=deps))
        for b in self.all_bufs:
            b.writer = None
            b.readers = []
        return bar

    def finalize_counts(self):
        for e in ENGS:
            c = 0
            for o in self.ops[e]:
                if o.dma_key is None and o.sig and o.fn is not None:
                    c += 1
                o.count = c


def build_nc(stop_after=None, debug=False, skip=(), b1_limit=None, dump=False):
    nc = bass.Bass("TRN2", target_bir_lowering=False)
    okind = "ExternalOutput" if debug else "Internal"

    def din(name, shape, dt=F32):
        return nc.dram_tensor(name, list(shape), dt, kind="ExternalInput").ap()

    xT_in = din("xT", [D, S])
    w_in = din("w_in", [DEPTH, 128, 8, WIN_COLS])
    w_out = din("w_out", [DEPTH, 128, 8, D])
    w_gate = din("w_gate", [DEPTH, 128, 8, DFF])
    w_up = din("w_up", [DEPTH, 128, 8, DFF])
    w_down = din("w_down", [DEPTH, 128, NF, D])
    g_mix = din("g_mix", [DEPTH, 128, 8])
    g_ffn = din("g_ffn", [DEPTH, 128, 8])
    g_fin = din("g_fin", [128, 8])
    g_out = din("g_out", [DEPTH, 128, D])
    sinkb = din("sinkb", [DEPTH, 128, 8])
    qscale_d = din("qscale", [128, NQK])
    ident_d = din("ident", [128, 128], BF16)
    identf_d = din("identf", [128, 128], F32)
    ones_d = din("ones", [128, 128], BF16)
    ddt_d = din("ddt", [9, 128, 512], BF16)
    dw_d = din("dw", [128, 384], BF16)

    out_d = nc.dram_tensor("out", [S, D], F32, kind="ExternalOutput").ap()
    XS = nc.dram_tensor("xs", [D, S], F32, kind=okind).ap()
    QKT = nc.dram_tensor("qkt", [NQK * 128, S], BF16, kind=okind).ap()
    VS = nc.dram_tensor("vs", [S, VW], BF16, kind=okind).ap()
    OP = nc.dram_tensor("op", [4, S, 520], F32, kind=okind).ap()
    if debug:
        DBG_YN = nc.dram_tensor("dbg_yn", [S, D], BF16, kind=okind).ap()
        DBG_Y = nc.dram_tensor("dbg_y", [S, D], F32, kind=okind).ap()

    sch = Sched()
    stack = ExitStack()
    NB16 = 106000
    SB = stack.enter_context(nc.sbuf_tensor("sb", [128, NB16], BF16))
    SBF = SB.bitcast(F32)
    PS = [stack.enter_context(nc.psum_tensor(f"ps{i}", [128, 512], F32)) for i in range(8)]
    PSB = [p.bitcast(BF16) for p in PS]

    class Arena:
        def __init__(self, base=0):
            self.off = base

        def alloc(self, dt, *free):
            n = int(np.prod(free))
            sz = 2 if dt == BF16 else 4
            self.off = (self.off + 63) // 64 * 64
            off = self.off
            self.off += n * sz
            assert self.off <= NB16 * 2, f"SBUF overflow {self.off}"
            if dt == BF16:
                ap = SB[:, off // 2: off // 2 + n]
            else:
                ap = SBF[:, off // 4: off // 4 + n]
            if len(free) == 2:
                ap = ap.rearrange("p (a b) -> p a b", a=free[0], b=free[1])
            elif len(free) == 3:
                ap = ap.rearrange("p (a b c) -> p a b c", a=free[0], b=free[1], c=free[2])
            return ap

    def sub_ap(ap, part0, nparts, col_off, dims):
        pstride = ap.ap[0][0]
        return bass.AP(ap.tensor, ap.offset + part0 * pstride + col_off, [[pstride, nparts]] + dims)

    ar0 = Arena(0)
    ident = ar0.alloc(BF16, 128)
    identf = ar0.alloc(F32, 128)
    ones = ar0.alloc(BF16, 128)
    dwt = ar0.alloc(BF16, 384)
    qscale = ar0.alloc(F32, NQK)
    gfin = ar0.alloc(F32, 8)
    epsc = ar0.alloc(F32, 1)
    b_const = sch.buf("const")
    CONST_END = ar0.off

    def dma(eng, out, in_, key, reads=(), writes=()):
        return sch.op(eng, lambda e: e.dma_start(out=out, in_=in_), reads=reads, writes=writes, dma_key=key)

    dma("sp", ident, ident_d, "const", writes=[b_const])
    dma("sp", identf, identf_d, "const", writes=[b_const])
    dma("sp", ones, ones_d, "const", writes=[b_const])
    dma("sp", dwt, dw_d, "const", writes=[b_const])
    dma("sp", qscale, qscale_d, "const", writes=[b_const])
    dma("sp", gfin, g_fin, "const", writes=[b_const])
    sch.op("dve", lambda e: e.memset(epsc, EPS), writes=[b_const])
    sch.barrier()

    def xsrc(l):
        return xT_in if l == 0 else XS

    def load_x_tile(dst, src, it):
        return src.rearrange("(c p) t -> p c t", p=128)[:, :, it * 512:(it + 1) * 512]

    def emit_rstd(ps_buf, ps_ap, rstd_buf, rstd_ap, tmp_ap, n, tmp_buf):
        sch.op("act", lambda e: e.activation(out=tmp_ap, in_=ps_ap, func=AF.Sqrt, scale=1.0 / n, bias=epsc),
               reads=[ps_buf, b_const], writes=[tmp_buf])
        return sch.op("dve", lambda e: e.reciprocal(out=rstd_ap, in_=tmp_ap), reads=[tmp_buf], writes=[rstd_buf])

    def phase_A(l):
        ar = Arena(CONST_END)
        wA = ar.alloc(BF16, 8, WIN_COLS)
        gm = ar.alloc(F32, 8)
        xt = [ar.alloc(F32, 8, 512) for _ in range(2)]
        sq = [ar.alloc(BF16, 8, 512) for _ in range(2)]
        hT = [ar.alloc(BF16, 8, 512) for _ in range(2)]
        rstd = [ar.alloc(F32, 512) for _ in range(2)]
        rtmp = [ar.alloc(F32, 512) for _ in range(2)]
        stg = [ar.alloc(BF16, NQK, 512) for _ in range(2)]
        vst = [ar.alloc(BF16, 4, 10, 65) for _ in range(2)]
        b_w = sch.buf("wA")
        b_gm = sch.buf("gm")
        b_xt = [sch.buf("xt0"), sch.buf("xt1")]
        b_sq = [sch.buf("sq0"), sch.buf("sq1")]
        b_hT = [sch.buf("hT0"), sch.buf("hT1")]
        b_rstd = [sch.buf("rstd0"), sch.buf("rstd1")]
        b_rtmp = [sch.buf("rtmp0"), sch.buf("rtmp1")]
        b_stg = [sch.buf("stg0"), sch.buf("stg1")]
        b_vst = [sch.buf("vst0"), sch.buf("vst1")]
        b_ps = [sch.buf(f"psA{i}") for i in range(8)]

        src = xsrc(l)
        NT = S // 512
        dma("sp", xt[0], load_x_tile(None, src, 0), "xt0", writes=[b_xt[0]])
        dma("sp", gm, g_mix[l], "gm", writes=[b_gm])
        WBLK = [(0, 512), (512, 1024), (1024, 1792), (1792, WIN_COLS)]
        b_wb = [sch.buf(f"wA{k}") for k in range(4)]
        for k, (c0_, c1_) in enumerate(WBLK):
            for c in range(8):
                dma("pool", wA[:, c, c0_:c1_], w_in[l, :, c, c0_:c1_], f"wA{k}", writes=[b_wb[k]])
        for s in range(2):
            sch.op("dve", lambda e, s=s: e.memset(vst[s][:, :, :, 64:65], 1.0), writes=[b_vst[s]])

        def norm(it):
            s = it % 2
            sch.op("act", lambda e, s=s: e.activation(out=sq[s], in_=xt[s], func=AF.Square),
                   reads=[b_xt[s]], writes=[b_sq[s]])

            def f_ssq(e, s=s):
                ins = None
                for c in range(8):
                    ins = e.matmul(PS[7][:, :], lhsT=ones, rhs=sq[s][:, c, :], start=(c == 0), stop=(c == 7))
                return ins
            sch.op("pe", f_ssq, reads=[b_sq[s], b_const], writes=[b_ps[7]])
            emit_rstd(b_ps[7], PS[7][:, :], b_rstd[s], rstd[s], rtmp[s], D, b_rtmp[s])

            def f_h(e, s=s):
                ins = None
                for c in range(8):
                    ins = e.scalar_tensor_tensor(out=hT[s][:, c, :], in0=xt[s][:, c, :], scalar=gm[:, c:c + 1],
                                                 in1=rstd[s], op0=ALU.mult, op1=ALU.mult)
                return ins
            sch.op("dve", f_h, reads=[b_xt[s], b_rstd[s], b_gm], writes=[b_hT[s]])

        def qk_chunk(it, j):
            s = it % 2
            pb = j % 4

            def f_mm(e, j=j, pb=pb, s=s):
                ins = None
                for c in range(8):
                    ins = e.matmul(PS[pb][:, :], lhsT=wA[:, c, j * 128:(j + 1) * 128], rhs=hT[s][:, c, :],
                                   start=(c == 0), stop=(c == 7))
                return ins
            sch.op("pe", f_mm, reads=[b_hT[s], b_wb[0 if j < 4 else (1 if j < 8 else 2)]], writes=[b_ps[pb]])
            is_q = j < 4 or 8 <= j < 12
            if is_q:
                sch.op("dve", lambda e, j=j, pb=pb, s=s: e.tensor_scalar(
                    out=stg[s][:, j, :], in0=PS[pb][:, :], scalar1=qscale[:, j:j + 1], scalar2=None,
                    op0=ALU.mult), reads=[b_ps[pb], b_const], writes=[b_stg[s]])
            else:
                sch.op("act", lambda e, j=j, pb=pb, s=s: e.activation(
                    out=stg[s][:, j, :], in_=PS[pb][:, :], func=AF.Copy),
                    reads=[b_ps[pb]], writes=[b_stg[s]])

        norm(0)
        for it in range(NT):
            s = it % 2
            if it + 1 < NT:
                dma("sp", xt[1 - s], load_x_tile(None, src, it + 1), f"xt{1 - s}", writes=[b_xt[1 - s]])
            for j in range(7):
                qk_chunk(it, j)
            if it + 1 < NT:
                norm(it + 1)
            for j in range(7, NQK):
                qk_chunk(it, j)
            dma("pool", QKT.rearrange("(j p) t -> p j t", p=128)[:, :, it * 512:(it + 1) * 512], stg[s],
                f"stgst{s}", reads=[b_stg[s]])
            for sub in range(4):
                pa = 4 + (sub % 2)

                def f_v(e, sub=sub, pa=pa, s=s):
                    ins = None
                    for c in range(8):
                        ins = e.matmul(PS[pa][:, :], lhsT=hT[s][:, c, sub * 128:(sub + 1) * 128],
                                       rhs=wA[:, c, 1792:2304], start=(c == 0), stop=(c == 7))
                    return ins
                sch.op("pe", f_v, reads=[b_hT[s], b_wb[3]], writes=[b_ps[pa]])
                sch.op("act", lambda e, sub=sub, pa=pa, s=s: e.activation(
                    out=vst[s][:, sub, 0:8, 0:64], in_=PS[pa][:, :].rearrange("p (h d) -> p h d", h=8),
                    func=AF.Copy), reads=[b_ps[pa]], writes=[b_vst[s]])

                def f_v2(e, sub=sub, s=s):
                    ins = None
                    for c in range(8):
                        ins = e.matmul(PS[6][:, 0:128], lhsT=hT[s][:, c, sub * 128:(sub + 1) * 128],
                                       rhs=wA[:, c, 2304:2432], start=(c == 0), stop=(c == 7))
                    return ins
                sch.op("pe", f_v2, reads=[b_hT[s], b_wb[3]], writes=[b_ps[6]])
                sch.op("dve", lambda e, sub=sub, s=s: e.tensor_copy(
                    out=vst[s][:, sub, 8:10, 0:64], in_=PS[6][:, 0:128].rearrange("p (h d) -> p h d", h=2)),
                    reads=[b_ps[6]], writes=[b_vst[s]])
            dma("pool", VS[it * 512:(it + 1) * 512, :].rearrange("(u p) w -> p u w", p=128),
                vst[s].rearrange("p u h d -> p u (h d)"), f"vstst{s}", reads=[b_vst[s]])
        sch.barrier()

    WG_OFF, WU_OFF, WD_OFF = 76800, 121856, 166912
    assert WD_OFF + NF * D * 2 <= NB16 * 2

    def at(off, dt, *free):
        a_ = Arena(off)
        return a_.alloc(dt, *free)

    wg = at(WG_OFF, BF16, 8, DFF)
    wu = at(WU_OFF, BF16, 8, DFF)
    wd = at(WD_OFF, BF16, NF, D)
    b_wC = sch.buf("wC")
    LOOK = 2

    def prefetch_list(l):
        lst = [(wu[:, c, :], w_up[l, :, c, :]) for c in range(8)]
        lst += [(wd[:, f, :], w_down[l, :, f, :]) for f in range(NF)]
        return lst

    def prefetch_C(l, which):
        if which == "g":
            for c in range(2, 8):
                dma("pool", wg[:, c, :], w_gate[l, :, c, :], "wC", writes=[b_wC])
        if which == "g01":
            for c in range(2):
                dma("pool", wg[:, c, :], w_gate[l, :, c, :], "wC", writes=[b_wC])

    def phase_Bw(l):
        ar = Arena(CONST_END)
        QA = ar.alloc(BF16, 4, S)
        KA = ar.alloc(BF16, 4, S)
        QB = ar.alloc(BF16, 4, S)
        KB = ar.alloc(BF16, 2, S)
        VB = ar.alloc(BF16, 32, 130)
        pTw = [ar.alloc(BF16, 384) for _ in range(4)]
        ostw = [ar.alloc(F32, 520) for _ in range(2)]
        b_qk = sch.buf("qkB")
        b_qkA = sch.buf("qkA")
        b_VB = sch.buf("VB")
        b_pTw = [sch.buf(f"pTw{i}") for i in range(2)]
        b_Sw = [sch.buf(f"Sw{i}") for i in range(2)]
        b_oB = [sch.buf(f"oB{i}") for i in range(4)]
        b_ostw = [sch.buf(f"ostw{i}") for i in range(2)]
        qkv = QKT.rearrange("(j p) t -> p j t", p=128)
        for j in range(4):
            dma("sp", QB[:, j, :], qkv[:, 8 + j, :], "qkB", writes=[b_qk])
        for j in range(2):
            dma("sp", KB[:, j, :], qkv[:, 12 + j, :], "qkB", writes=[b_qk])
        for q4 in range(4):
            src = bass.AP(VS.tensor, VS.offset + q4 * 1024 * VW + 520, [[VW, 128], [VW * 128, 8], [1, 130]])
            dma("sp", VB[:, q4 * 8:(q4 + 1) * 8, :], src, "VB", writes=[b_VB])
        for j in range(4):
            dma("sp", QA[:, j, :], qkv[:, j, :], "qkA", writes=[b_qkA])
            dma("sp", KA[:, j, :], qkv[:, 4 + j, :], "qkA", writes=[b_qkA])

        NTT = S // 128
        items = [(i, c) for i in range(NTT) for c in range(4)]
        N = len(items)
        LK = 1

        def jbs_of(i):
            return [jb for jb in range(3) if 0 <= i - 1 + jb < NTT]

        for k in range(N + LK):
            if k < N:
                i, c = items[k]
                g = c // 2
                slot = k % 2
                jbs = jbs_of(i)
                c0, c1 = jbs[0] * 128, jbs[-1] * 128 + 128

                def f_s(e, i=i, c=c, g=g, slot=slot, jbs=jbs, c0=c0, c1=c1):
                    ins = None
                    for hh in range(2):
                        e.matmul(PS[2 * slot + hh][:, c0:c1], lhsT=ident, rhs=dwt[:, c0:c1], start=True, stop=False)
                    for jb in jbs:
                        j = i - 1 + jb
                        for hh in range(2):
                            ksel = 0 if g == hh else 1
                            kk = sub_ap(KB, 64 * hh, 64, ksel * S + j * 128, [[1, 128]])
                            qv = sub_ap(QB, 64 * hh, 64, c * S + i * 128, [[1, 128]])
                            ins = e.matmul(PS[2 * slot + hh][:, jb * 128:(jb + 1) * 128], lhsT=kk, rhs=qv,
                                           start=False, stop=(jb == jbs[-1]))
                    return ins
                sch.op("pe", f_s, reads=[b_qk, b_const], writes=[b_Sw[slot]])

                def f_e(e, c=c, slot=slot, c0=c0, c1=c1):
                    ins = None
                    for hh in range(2):
                        ins = e.activation(out=pTw[2 * slot + hh][:, c0:c1], in_=PS[2 * slot + hh][:, c0:c1],
                                           func=AF.Exp, scale=float(SL_WIN[2 * c + hh]))
                    return ins
                sch.op("act", f_e, reads=[b_Sw[slot]], writes=[b_pTw[slot]])
            kk_ = k - LK
            if kk_ >= 0:
                i, c = items[kk_]
                g = c // 2
                slot = kk_ % 2
                jbs = jbs_of(i)
                ob = (i % 2) * 2 + c // 2

                def f_pv(e, slot=slot, jbs=jbs, i=i, g=g, c=c, ob=ob):
                    ins = None
                    for hh in range(2):
                        h = 2 * c + hh
                        ops_ = PS[4 + ob][:, (h % 4) * 65:(h % 4) * 65 + 65]
                        for n_, jb in enumerate(jbs):
                            j = i - 1 + jb
                            ins = e.matmul(ops_, lhsT=pTw[2 * slot + hh][:, jb * 128:(jb + 1) * 128],
                                           rhs=VB[:, j, g * 65:(g + 1) * 65], start=(n_ == 0),
                                           stop=(n_ == len(jbs) - 1))
                    return ins
                sch.op("pe", f_pv, reads=[b_pTw[slot], b_VB], writes=[b_oB[ob]])
                if c == 3:
                    os_ = i % 2
                    ob0 = (i % 2) * 2

                    def f_ev(e, os_=os_, ob0=ob0):
                        e.tensor_copy(out=ostw[os_][:, 0:260], in_=PS[4 + ob0][:, 0:260])
                        return e.tensor_copy(out=ostw[os_][:, 260:520], in_=PS[5 + ob0][:, 0:260])
                    sch.op("dve", f_ev, reads=[b_oB[ob0], b_oB[ob0 + 1]], writes=[b_ostw[os_]])
                    dma("pool", OP[3, i * 128:(i + 1) * 128, :], ostw[os_], f"ostw{os_}", reads=[b_ostw[os_]])
        sch.barrier()
        return QA, KA

    def phase_Bd(l, QA, KA):
        pf = prefetch_list(l)
        ar = Arena(CONST_END + 2 * 4 * S * 2)
        vown = [ar.alloc(BF16, 8, 65) for _ in range(3)]
        vcmp = [ar.alloc(BF16, 8, 65) for _ in range(3)]
        pT = [ar.alloc(BF16, 512) for _ in range(4)]
        KC = [ar.alloc(BF16, 4, 128) for _ in range(3)]
        ost = [ar.alloc(F32, 520) for _ in range(2)]
        ddt = ar.alloc(BF16, 9, 512)
        LOOK = 1
        assert ar.off <= WU_OFF
        b_ddt = sch.buf("ddt")
        dma("sp", ddt, ddt_d.rearrange("a p c -> p a c"), "ddt", writes=[b_ddt])
        b_qk = sch.buf("qkA2")
        b_kc = [sch.buf(f"kc{i}") for i in range(3)]
        b_vown = [sch.buf(f"vown{i}") for i in range(3)]
        b_vcmp = [sch.buf(f"vcmp{i}") for i in range(3)]
        b_pT = [sch.buf(f"pT{i}") for i in range(4)]
        b_ost = [sch.buf(f"ost{i}") for i in range(2)]
        b_S = [sch.buf(f"S{i}") for i in range(4)]
        b_o = [sch.buf(f"o{i}") for i in range(4)]

        tiles = []
        for p, dil in enumerate(DILS):
            L = S // dil
            for r in range(dil):
                for m in range(L // 128):
                    tiles.append((p, dil, L, r, m))
        if b1_limit is not None:
            tiles = [tiles[k] for k in b1_limit]

        def geom(ti):
            p, dil, L, r, m = tiles[ti]
            nm = L // 128
            la = 128 * m - 64 if m > 0 else 0
            lb = 128 * m + 128 if m < nm - 1 else 128 * m
            var = 1 if m == 0 else (2 if m == nm - 1 else 0)
            t0 = r + dil * 128 * m
            return p, dil, r, m, la, lb, var, t0

        def v_loads(ti):
            slot = ti % 3
            p, dil, r, m, la, lb, var, t0 = geom(ti)
            src = bass.AP(VS.tensor, VS.offset + t0 * VW, [[dil * VW, 128], [1, 520]])
            dma("sp", vown[slot].rearrange("p h d -> p (h d)"), src, f"vown{slot}", writes=[b_vown[slot]])
            for half, l0 in ((0, la), (1, lb)):
                src = bass.AP(VS.tensor, VS.offset + (r + dil * l0) * VW, [[dil * VW, 64], [1, 520]])
                dma("sp", vcmp[slot][64 * half:64 * half + 64].rearrange("p h d -> p (h d)"), src,
                    f"vcmp{slot}", writes=[b_vcmp[slot]])

            def f_kc(e, slot=slot, r=r, dil=dil, la=la, lb=lb):
                ins = None
                for half, l0 in ((0, la), (1, lb)):
                    ins = e.tensor_copy(out=KC[slot][:, :, 64 * half:64 * half + 64],
                                        in_=sub_ap(KA, 0, 128, r + dil * l0, [[S, 4], [dil, 64]]))
                return ins
            sch.op("dve", f_kc, reads=[b_qk], writes=[b_kc[slot]])

        for ti in range(min(3, len(tiles))):
            v_loads(ti)
        items = [(ti, c) for ti in range(len(tiles)) for c in range(4)]
        N = len(items)
        LOOK = 1
        for k in range(N + LOOK):
            if k < N:
                ti, c = items[k]
                p, dil, r, m, la, lb, var, t0 = geom(ti)
                slot = ti % 3
                sslot = k % 2

                def f_s(e, sslot=sslot, slot=slot, c=c, p=p, var=var, t0=t0, dil=dil):
                    for hh in range(2):
                        e.matmul(PS[2 * sslot + hh][:, 0:256], lhsT=ident, rhs=ddt[:, p * 3 + var, 0:256],
                                 start=True, stop=False)
                    ins = None
                    for part in range(2):
                        for hh in range(2):
                            qv = sub_ap(QA, 64 * hh, 64, c * S + t0, [[dil, 128]])
                            if part == 0:
                                kx = sub_ap(KA, 64 * hh, 64, c * S + t0, [[dil, 128]])
                            else:
                                kx = KC[slot][64 * hh:64 * hh + 64, c, :]
                            ins = e.matmul(PS[2 * sslot + hh][:, part * 128:part * 128 + 128], lhsT=kx, rhs=qv,
                                           start=False, stop=(part == 1))
                    return ins
                sch.op("pe", f_s, reads=[b_qk, b_const, b_kc[slot], b_ddt], writes=[b_S[sslot]])

                def f_e(e, sslot=sslot, c=c):
                    ins = None
                    for hh in range(2):
                        ins = e.activation(out=pT[sslot][:, hh * 256:(hh + 1) * 256],
                                           in_=PS[2 * sslot + hh][:, 0:256], func=AF.Exp,
                                           scale=float(SL_DIL[2 * c + hh]))
                    return ins
                sch.op("act", f_e, reads=[b_S[sslot]], writes=[b_pT[sslot]])
            kk_ = k - LOOK
            if kk_ >= 0:
                ti, c = items[kk_]
                p, dil, r, m, la, lb, var, t0 = geom(ti)
                slot = ti % 3
                pslot = kk_ % 2
                ob = (ti % 2) * 2
                obank = 4 + ob + c // 2

                def f_pv(e, pslot=pslot, slot=slot, c=c, obank=obank):
                    ins = None
                    for hh in range(2):
                        h = 2 * c + hh
                        ops_ = PS[obank][:, (h % 4) * 65:(h % 4) * 65 + 65]
                        e.matmul(ops_, lhsT=pT[pslot][:, hh * 256:hh * 256 + 128], rhs=vown[slot][:, h, :],
                                 start=True, stop=False)
                        ins = e.matmul(ops_, lhsT=pT[pslot][:, hh * 256 + 128:hh * 256 + 256],
                                       rhs=vcmp[slot][:, h, :], start=False, stop=True)
                    return ins
                sch.op("pe", f_pv, reads=[b_pT[pslot], b_vown[slot], b_vcmp[slot]], writes=[b_o[ob + c // 2]])
                if c == 3:
                    os_ = ti % 2

                    def f_ev(e, os_=os_, ob=ob):
                        e.tensor_copy(out=ost[os_][:, 0:260], in_=PS[4 + ob][:, 0:260])
                        return e.tensor_copy(out=ost[os_][:, 260:520], in_=PS[5 + ob][:, 0:260])
                    sch.op("dve", f_ev, reads=[b_o[ob], b_o[ob + 1]], writes=[b_ost[os_]])
                    dst = bass.AP(OP.tensor, OP.offset + p * S * 520 + t0 * 520, [[dil * 520, 128], [1, 520]])
                    dma("sp", dst, ost[os_], f"ostst{os_}", reads=[b_ost[os_]])
                    if ti + 3 < len(tiles):
                        v_loads(ti + 3)
                    if pf:
                        o_, i_ = pf.pop(0)
                        dma("pool", o_, i_, "wC", writes=[b_wC])
        while pf:
            o_, i_ = pf.pop(0)
            dma("pool", o_, i_, "wC", writes=[b_wC])
        sch.barrier()

    def phase_Be(l):
        ar = Arena(CONST_END)
        wo = ar.alloc(BF16, 8, D)
        gob = ar.alloc(F32, D)
        esink = ar.alloc(F32, 8)
        NOS = 3
        osl = [[ar.alloc(F32, 520) for _ in range(4)] for _ in range(NOS)]
        ya = [ar.alloc(F32, 512) for _ in range(2)]
        yb = [ar.alloc(F32, 512) for _ in range(2)]
        sm = [ar.alloc(F32, 32) for _ in range(2)]
        yn = [ar.alloc(BF16, D) for _ in range(2)]
        ynT = ar.alloc(BF16, 8, 512)
        xtb = ar.alloc(F32, 8, 512)
        assert ar.off <= WG_OFF + 2 * DFF * 2, ar.off
        b_wo = sch.buf("wo")
        b_small = sch.buf("smallB")
        b_osl = [sch.buf(f"osl{i}") for i in range(NOS)]
        b_ep = [sch.buf("ep0"), sch.buf("ep1")]
        b_yn = [sch.buf("yn0"), sch.buf("yn1")]
        b_ynT = sch.buf("ynT")
        b_xtb = sch.buf("xtB")
        b_tp = [sch.buf("tp0"), sch.buf("tp1")]
        b_op = [sch.buf("opj0"), sch.buf("opj1")]
        for c in range(8):
            dma("pool", wo[:, c, :], w_out[l, :, c, :], "wo", writes=[b_wo])
        prefetch_C(l, "g")
        dma("sp", gob, g_out[l], "smallB", writes=[b_small])
        dma("sp", esink, sinkb[l], "smallB", writes=[b_small])
        sch.op("act", lambda e: e.activation(out=esink, in_=esink, func=AF.Exp), reads=[b_small], writes=[b_small])

        def o_loads(i):
            s = i % NOS
            for p in range(4):
                dma("sp", osl[s][p], OP[p, i * 128:(i + 1) * 128, :], f"osl{s}", writes=[b_osl[s]])

        def bc(ap8, n, stride=1):
            return bass.AP(ap8.tensor, ap8.offset, [list(ap8.ap[0]), [stride, n], [0, 64]])

        src = xsrc(l)
        NTT = S // 128

        def stage_X(i):
            s = i % 2
            so = i % NOS
            o0, o1, o2, o3 = osl[so]
            o0v = o0.rearrange("p (h d) -> p h d", h=8)
            o3v = o3.rearrange("p (h d) -> p h d", h=8)
            sm_, ya_, yb_, yn_ = sm[s], ya[s], yb[s], yn[s]

            def f_1(e):
                e.memset(sm_[:, 24:26], 0.0)
                e.tensor_tensor(out=sm_[:, 8:16], in0=o3v[:, :, 64], in1=esink, op=ALU.add)
                return e.tensor_tensor(out=o0, in0=o0, in1=o1, op=ALU.add)
            sch.op("dve", f_1, reads=[b_osl[so], b_small], writes=[b_osl[so], b_ep[s]])

            def f_2(e):
                e.reciprocal(out=sm_[:, 16:24], in_=sm_[:, 8:16])
                return e.tensor_tensor(out=o0, in0=o0, in1=o2, op=ALU.add)
            sch.op("dve", f_2, reads=[b_osl[so], b_ep[s]], writes=[b_osl[so], b_ep[s]])

            def f_3(e):
                e.reciprocal(out=sm_[:, 0:8], in_=o0v[:, :, 64])
                return e.tensor_tensor(out=yb_.rearrange("p (h d) -> p h d", h=8), in0=o3v[:, :, 0:64],
                                       in1=bc(sm_[:, 16:24], 8), op=ALU.mult)
            sch.op("dve", f_3, reads=[b_osl[so], b_ep[s]], writes=[b_ep[s]])
            sch.op("dve", lambda e: e.tensor_tensor(
                out=ya_.rearrange("p (h d) -> p h d", h=8), in0=o0v[:, :, 0:64], in1=bc(sm_[:, 0:8], 8),
                op=ALU.mult), reads=[b_osl[so], b_ep[s]], writes=[b_ep[s]])
            if i + NOS < NTT:
                o_loads(i + NOS)

            def f_ss(e):
                e.activation(out=yn_[:, 0:512], in_=ya_, func=AF.Square, accum_out=sm_[:, 24:25])
                return e.activation(out=yn_[:, 512:1024], in_=yb_, func=AF.Square, accum_out=sm_[:, 25:26])
            sch.op("act", f_ss, reads=[b_ep[s]], writes=[b_ep[s], b_yn[s]])
            sch.op("act", lambda e: e.activation(out=sm_[:, 26:28], in_=sm_[:, 24:26], func=AF.Sqrt, scale=1.0 / 512,
                                                 bias=epsc), reads=[b_ep[s], b_const], writes=[b_ep[s]])

        def stage_Y(i):
            s = i % 2
            big = i // 4
            sub = i % 4
            sm_, ya_, yb_, yn_ = sm[s], ya[s], yb[s], yn[s]
            if sub == 0:
                dma("sp", xtb, load_x_tile(None, src, big), "xtB0", writes=[b_xtb])

            sch.op("dve", lambda e: e.reciprocal(out=sm_[:, 28:30], in_=sm_[:, 26:28]),
                   reads=[b_ep[s]], writes=[b_ep[s]])

            def f_4(e):
                e.scalar_tensor_tensor(out=yn_[:, 0:512], in0=ya_, scalar=sm_[:, 28:29], in1=gob[:, 0:512],
                                       op0=ALU.mult, op1=ALU.mult)
                return e.scalar_tensor_tensor(out=yn_[:, 512:1024], in0=yb_, scalar=sm_[:, 29:30],
                                              in1=gob[:, 512:1024], op0=ALU.mult, op1=ALU.mult)
            sch.op("dve", f_4, reads=[b_ep[s], b_small], writes=[b_yn[s]])
            if debug and dump:
                dma("pool", DBG_YN[i * 128:(i + 1) * 128, :], yn_, f"dbg0{s}", reads=[b_yn[s]])
                dma("pool", DBG_Y[i * 128:(i + 1) * 128, 0:512], ya_, f"dbg1{s}", reads=[b_ep[s]])
                dma("pool", DBG_Y[i * 128:(i + 1) * 128, 512:1024], yb_, f"dbg2{s}", reads=[b_ep[s]])
            tb = i % 2

            def f_tp(e, tb=tb):
                ins = None
                for c in range(8):
                    ins = e.transpose(out=PSB[6 + tb][:, c * 128:(c + 1) * 128], in_=yn_[:, c * 128:(c + 1) * 128],
                                      identity=ident)
                return ins
            sch.op("pe", f_tp, reads=[b_yn[s], b_const], writes=[b_tp[tb]])
            sch.op("act", lambda e, sub=sub, tb=tb: e.activation(
                out=ynT[:, :, sub * 128:(sub + 1) * 128],
                in_=PSB[6 + tb][:, 0:1024].rearrange("p (c t) -> p c t", c=8), func=AF.Copy),
                reads=[b_tp[tb]], writes=[b_ynT])
            if sub == 3:
                for d in range(8):
                    pb = d % 2

                    def f_o(e, d=d, pb=pb):
                        ins = None
                        for c in range(8):
                            ins = e.matmul(PS[4 + pb][:, :], lhsT=wo[:, c, d * 128:(d + 1) * 128], rhs=ynT[:, c, :],
                                           start=(c == 0), stop=(c == 7))
                        return ins
                    sch.op("pe", f_o, reads=[b_ynT, b_wo], writes=[b_op[pb]])
                    sch.op("dve", lambda e, d=d, pb=pb: e.tensor_tensor(
                        out=xtb[:, d, :], in0=PS[4 + pb][:, :], in1=xtb[:, d, :], op=ALU.add),
                        reads=[b_op[pb], b_xtb], writes=[b_xtb])
                dma("pool", XS.rearrange("(c p) t -> p c t", p=128)[:, :, big * 512:(big + 1) * 512], xtb,
                    "xtBst0", reads=[b_xtb])

        for i0 in range(NOS):
            o_loads(i0)
        stage_X(0)
        for i in range(NTT):
            if i + 1 < NTT:
                stage_X(i + 1)
            stage_Y(i)
        sch.barrier()

    def phase_B(l):
        QA, KA = phase_Bw(l)
        if stop_after == ("Bw", l):
            return
        phase_Bd(l, QA, KA)
        if stop_after == ("B1", l):
            return
        phase_Be(l)

    def phase_C(l):
        last = (l == DEPTH - 1)
        ar = Arena(CONST_END)
        gm = ar.alloc(F32, 8)
        xt = [ar.alloc(F32, 8, 512) for _ in range(2)]
        hT = ar.alloc(BF16, 8, 512)
        ar.off = (ar.off + 63) // 64 * 64
        aoff = ar.off
        actT = ar.alloc(BF16, NF, 512)
        sqf = actT[:, 0:8, :]
        ostf = [SBF[:, aoff // 4 + 2048 + k * 1024: aoff // 4 + 2048 + (k + 1) * 1024] for k in range(2)]
        rstd = ar.alloc(F32, 512)
        rstd2 = ar.alloc(F32, 512)
        sg = [ar.alloc(F32, 512) for _ in range(2)]
        assert ar.off <= WG_OFF, ar.off
        b_w = b_wC
        b_gm = sch.buf("gmC")
        b_xt = [sch.buf("xtC0"), sch.buf("xtC1")]
        b_hT = sch.buf("hTC")
        b_act = sch.buf("actT")
        b_rstd = sch.buf("rstdC")
        b_rstd2 = sch.buf("rstdC2")
        b_sg = [sch.buf("sg0"), sch.buf("sg1")]
        b_ps = [sch.buf(f"psC{i}") for i in range(8)]
        prefetch_C(l, "g01")
        dma("sp", gm, g_ffn[l], "gmC", writes=[b_gm])
        NT = S // 512
        dma("sp", xt[0], load_x_tile(None, XS, 0), "xtC0", writes=[b_xt[0]])

        def norm_sq(it):
            s = it % 2
            sch.op("act", lambda e, s=s: e.activation(out=hT, in_=xt[s], func=AF.Square),
                   reads=[b_xt[s]], writes=[b_hT])

        def norm_rest(it):
            s = it % 2

            def f_ssq(e):
                ins = None
                for c in range(8):
                    ins = e.matmul(PS[7][:, :], lhsT=ones, rhs=hT[:, c, :], start=(c == 0), stop=(c == 7))
                return ins
            sch.op("pe", f_ssq, reads=[b_hT, b_const], writes=[b_ps[7]])
            emit_rstd(b_ps[7], PS[7][:, :], b_rstd, rstd, sg[0], D, tmp_buf=b_sg[0])

            def f_h(e, s=s):
                ins = None
                for c in range(8):
                    ins = e.scalar_tensor_tensor(out=hT[:, c, :], in0=xt[s][:, c, :], scalar=gm[:, c:c + 1],
                                                 in1=rstd, op0=ALU.mult, op1=ALU.mult)
                return ins
            sch.op("dve", f_h, reads=[b_xt[s], b_rstd, b_gm], writes=[b_hT])

        def down(it, d):
            s = it % 2
            pd = 4 + d % 2

            def f_d(e, d=d, pd=pd):
                ins = None
                for f in range(NF):
                    ins = e.matmul(PS[pd][:, :], lhsT=wd[:, f, d * 128:(d + 1) * 128], rhs=actT[:, f, :],
                                   start=(f == 0), stop=(f == NF - 1))
                return ins
            sch.op("pe", f_d, reads=[b_act, b_w], writes=[b_ps[pd]])
            sch.op("dve", lambda e, d=d, pd=pd, s=s: e.tensor_tensor(
                out=xt[s][:, d, :], in0=PS[pd][:, :], in1=xt[s][:, d, :], op=ALU.add),
                reads=[b_ps[pd], b_xt[s]], writes=[b_xt[s]])

        norm_sq(0)
        norm_rest(0)
        for it in range(NT):
            s = it % 2
            if it + 1 < NT:
                dma("sp", xt[1 - s], load_x_tile(None, XS, it + 1), f"xtC{1 - s}", writes=[b_xt[1 - s]])
            for f in range(NF):
                pg = (f % 2) * 2

                def f_g(e, f=f, pg=pg):
                    for c in range(8):
                        e.matmul(PS[pg][:, :], lhsT=wg[:, c, f * 128:(f + 1) * 128], rhs=hT[:, c, :],
                                 start=(c == 0), stop=(c == 7))
                    ins = None
                    for c in range(8):
                        ins = e.matmul(PS[pg + 1][:, :], lhsT=wu[:, c, f * 128:(f + 1) * 128], rhs=hT[:, c, :],
                                       start=(c == 0), stop=(c == 7))
                    return ins
                sch.op("pe", f_g, reads=[b_hT, b_w], writes=[b_ps[pg], b_ps[pg + 1]])
                sch.op("act", lambda e, f=f, pg=pg: e.activation(out=sg[f % 2], in_=PS[pg][:, :], func=AF.Silu),
                       reads=[b_ps[pg]], writes=[b_sg[f % 2]])
                sch.op("dve", lambda e, f=f, pg=pg: e.tensor_tensor(
                    out=actT[:, f, :], in0=PS[pg + 1][:, :], in1=sg[f % 2], op=ALU.mult),
                    reads=[b_sg[f % 2], b_ps[pg + 1]], writes=[b_act])
            if it + 1 < NT:
                norm_sq(it + 1)
            for d in range(4):
                down(it, d)
            if it + 1 < NT:
                norm_rest(it + 1)
            for d in range(4, 8):
                down(it, d)
            if not last:
                dma("pool", XS.rearrange("(c p) t -> p c t", p=128)[:, :, it * 512:(it + 1) * 512], xt[s],
                    f"xtCst{s}", reads=[b_xt[s]])
            else:
                sch.op("act", lambda e, s=s: e.activation(out=sqf, in_=xt[s], func=AF.Square),
                       reads=[b_xt[s]], writes=[b_act])

                def f_ssq2(e):
                    ins = None
                    for c in range(8):
                        ins = e.matmul(PS[7][:, :], lhsT=ones, rhs=sqf[:, c, :], start=(c == 0), stop=(c == 7))
                    return ins
                sch.op("pe", f_ssq2, reads=[b_act, b_const], writes=[b_ps[7]])
                emit_rstd(b_ps[7], PS[7][:, :], b_rstd2, rstd2, sg[1], D, tmp_buf=b_sg[1])

                def f_fin(e, s=s):
                    ins = None
                    for c in range(8):
                        ins = e.scalar_tensor_tensor(out=xt[s][:, c, :], in0=xt[s][:, c, :], scalar=gfin[:, c:c + 1],
                                                     in1=rstd2, op0=ALU.mult, op1=ALU.mult)
                    return ins
                sch.op("dve", f_fin, reads=[b_xt[s], b_rstd2, b_const], writes=[b_xt[s]])
                for sub in range(4):
                    osb = sub % 2
                    ostg = ostf[osb]
                    for hb in range(2):
                        pb = (sub * 2 + hb) % 4

                        def f_t(e, sub=sub, hb=hb, pb=pb, s=s):
                            ins = None
                            for cc in range(4):
                                c = hb * 4 + cc
                                ins = e.transpose(out=PS[pb][:, cc * 128:(cc + 1) * 128],
                                                  in_=xt[s][:, c, sub * 128:(sub + 1) * 128], identity=identf)
                            return ins
                        sch.op("pe", f_t, reads=[b_xt[s], b_const], writes=[b_ps[pb]])
                        if hb == 0:
                            sch.op("act", lambda e, pb=pb, ostg=ostg: e.activation(
                                out=ostg[:, 0:512], in_=PS[pb][:, :], func=AF.Copy),
                                reads=[b_ps[pb]], writes=[b_act])
                        else:
                            sch.op("dve", lambda e, pb=pb, ostg=ostg: e.tensor_copy(
                                out=ostg[:, 512:1024], in_=PS[pb][:, :]),
                                reads=[b_ps[pb]], writes=[b_act])
                    o = dma("pool", out_d[it * 512 + sub * 128: it * 512 + (sub + 1) * 128, :], ostg,
                            f"outst{osb}", reads=[b_act])
                    out_dmas.append(o)
        sch.barrier()

    out_dmas = []

    done = False
    for l in range(DEPTH):
        if done:
            break
        if "A" not in skip:
            phase_A(l)
        if stop_after == ("A", l):
            break
        phase_B(l)
        if stop_after in (("Bw", l), ("B1", l), ("B", l)):
            break
        phase_C(l)
        if stop_after == ("C", l):
            break

    sch.finalize_counts()
    eng_sems = {e: stack.enter_context(nc.semaphore(f"s_{e}")) for e in ENGS}
    dma_sems = {k: stack.enter_context(nc.semaphore(f"d_{k}")) for k in sch.dma_counts}
    handles = {"pe": "tensor", "act": "scalar", "dve": "vector", "pool": "gpsimd", "sp": "sync"}

    def replay(en, e):
        waited = {}

        def wait(sem, val):
            key = sem.name
            if waited.get(key, 0) < val:
                e.wait_ge(sem, val)
                waited[key] = val

        for o in sch.ops[en]:
            for d in o.deps:
                if d.dma_key is not None:
                    wait(dma_sems[d.dma_key], d.dma_count)
                else:
                    if d.fn is None:
                        continue
                    if d.eng == en:
                        if en == "pe" or not SAME_ENGINE_SYNC:
                            continue
                    wait(eng_sems[d.eng], d.count)
            if o.fn is None:
                continue
            ins = o.fn(e)
            if o.dma_key is not None:
                ins.then_inc(dma_sems[o.dma_key], 16)
            elif o.sig:
                ins.then_inc(eng_sems[en], 1)
        if en == "sp":
            for k, n in sch.dma_counts.items():
                wait(dma_sems[k], n)

    with nc.Block() as block:
        @block.tensor
        def _(e):
            replay("pe", e)

        @block.scalar
        def _(e):
            replay("act", e)

        @block.vector
        def _(e):
            replay("dve", e)

        @block.gpsimd
        def _(e):
            replay("pool", e)

        @block.sync
        def _(e):
            replay("sp", e)
    stack.close()
    return nc


def _chunked(w, nchunk):
    K, N = w.shape
    return np.ascontiguousarray(w.reshape(nchunk, 128, N).transpose(1, 0, 2))


def _vecT(g):
    return np.ascontiguousarray(g.reshape(-1, 128).T)


def _const_tables():
    bf = ml_dtypes.bfloat16
    ident = np.eye(128, dtype=np.float32)
    ones = np.ones((128, 128), np.float32)
    k = np.arange(128)[:, None].astype(np.float32)
    q = np.arange(128)[None, :].astype(np.float32)
    dd_own = np.zeros((3, 128, 128), np.float32)
    dd_comp = np.zeros((3, 3, 128, 128), np.float32)
    for p, dil in enumerate(DILS):
        dist = np.abs(k - q)
        dd_own[p] = np.where(dist <= 64, -dil * dist, -BIG)
        i = np.arange(128)[:, None].astype(np.float32)
        j = np.arange(128)[None, :].astype(np.float32)
        d_prev = 64 + j - i
        d_next = 64 + i - j
        full = np.where(i < 64, d_prev, d_next)
        base = np.where(full <= 64, -dil * full, -BIG)
        mid = base.copy()
        first = base.copy()
        first[:64] = -BIG
        lastv = base.copy()
        lastv[64:] = -BIG
        dd_comp[p, 0], dd_comp[p, 1], dd_comp[p, 2] = mid, first, lastv
    dw = np.zeros((128, 384), np.float32)
    dprev = 128 + q - k
    dw[:, 0:128] = np.where(dprev <= 128, -dprev, -BIG)
    dw[:, 128:256] = -np.abs(k - q)
    dnext = 128 + k - q
    dw[:, 256:384] = np.where(dnext <= 128, -dnext, -BIG)
    qs = np.ones((128, NQK), np.float32)
    for j in range(4):
        for p_ in range(128):
            qs[p_, j] = 1.0 / (8.0 * SL_DIL[2 * j + p_ // 64])
            qs[p_, 8 + j] = 1.0 / (8.0 * SL_WIN[2 * j + p_ // 64])
    ddt = np.zeros((9, 128, 512), np.float32)
    for p in range(3):
        for v in range(3):
            ddt[p * 3 + v] = np.concatenate([dd_own[p], dd_comp[p, v], dd_own[p], dd_comp[p, v]], axis=1)
    return dict(ident=ident.astype(bf), identf=ident, ones=ones.astype(bf), ddt=ddt.astype(bf),
                dw=dw.astype(bf), qscale=qs)


def _prep_shared(g_mix, w_in, g_out_dil, g_out_win, sink, w_out, g_ffn, w_gate, w_up, w_down, g_final):
    f = np.float32
    w_in = np.asarray(w_in, f)
    ext = np.concatenate([w_in[:, :, 0:512], w_in[:, :, 512:1024], w_in[:, :, 1536:2048], w_in[:, :, 2048:2176],
                          w_in[:, :, 2112:2176], w_in[:, :, 2048:2112], w_in[:, :, 1024:1536],
                          w_in[:, :, 2176:2304]], axis=2)
    sh = dict(
        w_in=np.stack([_chunked(ext[l], 8) for l in range(DEPTH)]),
        w_out=np.stack([_chunked(np.asarray(w_out[l], f), 8) for l in range(DEPTH)]),
        w_gate=np.stack([_chunked(np.asarray(w_gate[l], f), 8) for l in range(DEPTH)]),
        w_up=np.stack([_chunked(np.asarray(w_up[l], f), 8) for l in range(DEPTH)]),
        w_down=np.stack([_chunked(np.asarray(w_down[l], f), NF) for l in range(DEPTH)]),
        g_mix=np.stack([_vecT(np.asarray(g_mix[l], f)) for l in range(DEPTH)]),
        g_ffn=np.stack([_vecT(np.asarray(g_ffn[l], f)) for l in range(DEPTH)]),
        g_fin=_vecT(np.asarray(g_final, f)),
        g_out=np.stack([np.ascontiguousarray(np.broadcast_to(
            np.concatenate([np.asarray(g_out_dil[l], f), np.asarray(g_out_win[l], f)])[None, :], (128, D)))
            for l in range(DEPTH)]),
        sinkb=np.stack([np.ascontiguousarray(np.broadcast_to(np.asarray(sink[l], f)[None, :], (128, 8)))
                        for l in range(DEPTH)]),
    )
    sh.update(_const_tables())
    return sh


_NC_CACHE = {}


def kernel(x, g_mix, w_in, g_out_dil, g_out_win, sink, w_out, g_ffn, w_gate, w_up, w_down, g_final):
    x = np.asarray(x, np.float32)
    sh = _prep_shared(g_mix, w_in, g_out_dil, g_out_win, sink, w_out, g_ffn, w_gate, w_up, w_down, g_final)
    if "nc" not in _NC_CACHE:
        _NC_CACHE["nc"] = build_nc()
    nc = _NC_CACHE["nc"]
    in_maps = []
    for b in range(NCORES):
        m = dict(sh)
        m["xT"] = np.ascontiguousarray(x[b].T)
        in_maps.append(m)
    res = run_bass_kernel_spmd(nc, in_maps, core_ids=list(range(NCORES)))
    return np.stack([np.asarray(r["out"], np.float32) for r in res.results], axis=0)
```

```python
import math
from contextlib import ExitStack

import numpy as np
import ml_dtypes

import concourse.bass as bass
import concourse.mybir as mybir
from concourse.bass_utils import run_bass_kernel_spmd

F32 = mybir.dt.float32
BF16 = mybir.dt.bfloat16
AF = mybir.ActivationFunctionType
ALU = mybir.AluOpType

S = 4096
D = 1024
DEPTH = 2
DFF = 2816
NF = DFF // 128
EPS = 1e-6
NCORES = 8
WIN_COLS = 2432
NQK = 14
VW = 650
BIG = 131072.0
DILS = (1, 4, 16)
SLOPES = np.array([2.0 ** (-8.0 * (i + 1) / 16) for i in range(16)], dtype=np.float32)
SL_WIN = SLOPES[:8]
SL_DIL = SLOPES[8:]

SAME_ENGINE_SYNC = True


class Buf:
    __slots__ = ("name", "writer", "readers")

    def __init__(self, name):
        self.name = name
        self.writer = None
        self.readers = []


class Op:
    __slots__ = ("eng", "fn", "deps", "sig", "count", "dma_key", "dma_count")

    def __init__(self, eng, fn, dma_key):
        self.eng = eng
        self.fn = fn
        self.deps = []
        self.sig = False
        self.count = 0
        self.dma_key = dma_key
        self.dma_count = 0


ENGS = ("pe", "act", "dve", "pool", "sp")


class Sched:
    def __init__(self):
        self.ops = {e: [] for e in ENGS}
        self.dma_counts = {}
        self.all_bufs = []

    def buf(self, name):
        b = Buf(name)
        self.all_bufs.append(b)
        return b

    def op(self, eng, fn, reads=(), writes=(), dma_key=None, extra_deps=()):
        o = Op(eng, fn, dma_key)
        deps = []
        seen = set()

        def add(d):
            if d is not None and id(d) not in seen:
                seen.add(id(d))
                deps.append(d)

        for b in reads:
            add(b.writer)
        for b in writes:
            w = b.writer
            if not (w is not None and dma_key is not None and w.dma_key == dma_key and w.eng == eng):
                add(w)
            for r in b.readers:
                add(r)
        for d in extra_deps:
            add(d)
        for b in reads:
            b.readers.append(o)
        for b in writes:
            b.writer = o
            b.readers = []
        o.deps = deps
        for d in deps:
            d.sig = True
        if dma_key is not None:
            n = self.dma_counts.get(dma_key, 0) + 16
            self.dma_counts[dma_key] = n
            o.dma_count = n
        self.ops[eng].append(o)
        return o

    def barrier(self):
        lasts = []
        for e in ENGS:
            if self.ops[e]:
                lasts.append(self.ops[e][-1])
        last_dma = {}
        for e in ENGS:
            for o in self.ops[e]:
                if o.dma_key is not None:
                    last_dma[o.dma_key] = o
        deps = lasts + list(last_dma.values())
        bar = []
        for e in ENGS:
            bar.append(self.op(e, None, extra_deps=deps))
        for b in self.all_bufs:
            b.writer = None
            b.readers = []
        return bar

    def finalize_counts(self):
        for e in ENGS:
            c = 0
            for o in self.ops[e]:
                if o.dma_key is None and o.sig and o.fn is not None:
                    c += 1
                o.count = c


def build_nc(stop_after=None, debug=False, skip=(), b1_limit=None, dump=False):
    nc = bass.Bass("TRN2", target_bir_lowering=False)
    okind = "ExternalOutput" if debug else "Internal"

    def din(name, shape, dt=F32):
        return nc.dram_tensor(name, list(shape), dt, kind="ExternalInput").ap()

    xT_in = din("xT", [D, S])
    w_in = din("w_in", [DEPTH, 128, 8, WIN_COLS])
    w_out = din("w_out", [DEPTH, 128, 8, D])
    w_gate = din("w_gate", [DEPTH, 128, 8, DFF])
    w_up = din("w_up", [DEPTH, 128, 8, DFF])
    w_down = din("w_down", [DEPTH, 128, NF, D])
    g_mix = din("g_mix", [DEPTH, 128, 8])
    g_ffn = din("g_ffn", [DEPTH, 128, 8])
    g_fin = din("g_fin", [128, 8])
    g_finb = din("g_finb", [128, D])
    g_out = din("g_out", [DEPTH, 128, D])
    sinkb = din("sinkb", [DEPTH, 128, 8])
    qscale_d = din("qscale", [128, NQK])
    ident_d = din("ident", [128, 128], BF16)
    identf_d = din("identf", [128, 128], F32)
    ones_d = din("ones", [128, 128], BF16)
    ddt_d = din("ddt", [9, 128, 512], BF16)
    dw_d = din("dw", [128, 384], BF16)

    out_d = nc.dram_tensor("out", [S, D], F32, kind="ExternalOutput").ap()
    XS = nc.dram_tensor("xs", [D, S], F32, kind=okind).ap()
    QKT = nc.dram_tensor("qkt", [NQK * 128, S], BF16, kind=okind).ap()
    VS = nc.dram_tensor("vs", [S, VW], BF16, kind=okind).ap()
    OP = nc.dram_tensor("op", [4, S, 520], F32, kind=okind).ap()
    if debug:
        DBG_YN = nc.dram_tensor("dbg_yn", [S, D], BF16, kind=okind).ap()
        DBG_Y = nc.dram_tensor("dbg_y", [S, D], F32, kind=okind).ap()

    sch = Sched()
    stack = ExitStack()
    NB16 = 106000
    SB = stack.enter_context(nc.sbuf_tensor("sb", [128, NB16], BF16))
    SBF = SB.bitcast(F32)
    PS = [stack.enter_context(nc.psum_tensor(f"ps{i}", [128, 512], F32)) for i in range(8)]
    PSB = [p.bitcast(BF16) for p in PS]

    class Arena:
        def __init__(self, base=0):
            self.off = base

        def alloc(self, dt, *free):
            n = int(np.prod(free))
            sz = 2 if dt == BF16 else 4
            self.off = (self.off + 63) // 64 * 64
            off = self.off
            self.off += n * sz
            assert self.off <= NB16 * 2, f"SBUF overflow {self.off}"
            if dt == BF16:
                ap = SB[:, off // 2: off // 2 + n]
            else:
                ap = SBF[:, off // 4: off // 4 + n]
            if len(free) == 2:
                ap = ap.rearrange("p (a b) -> p a b", a=free[0], b=free[1])
            elif len(free) == 3:
                ap = ap.rearrange("p (a b c) -> p a b c", a=free[0], b=free[1], c=free[2])
            return ap

    def sub_ap(ap, part0, nparts, col_off, dims):
        pstride = ap.ap[0][0]
        return bass.AP(ap.tensor, ap.offset + part0 * pstride + col_off, [[pstride, nparts]] + dims)

    ar0 = Arena(0)
    ident = ar0.alloc(BF16, 128)
    identf = ar0.alloc(F32, 128)
    ones = ar0.alloc(BF16, 128)
    dwt = ar0.alloc(BF16, 384)
    qscale = ar0.alloc(F32, NQK)
    gfin = ar0.alloc(F32, 8)
    epsc = ar0.alloc(F32, 1)
    b_const = sch.buf("const")
    CONST_END = ar0.off

    def dma(eng, out, in_, key, reads=(), writes=()):
        return sch.op(eng, lambda e: e.dma_start(out=out, in_=in_), reads=reads, writes=writes, dma_key=key)

    dma("sp", ident, ident_d, "const", writes=[b_const])
    dma("sp", identf, identf_d, "const", writes=[b_const])
    dma("sp", ones, ones_d, "const", writes=[b_const])
    dma("sp", dwt, dw_d, "const", writes=[b_const])
    dma("sp", qscale, qscale_d, "const", writes=[b_const])
    dma("sp", gfin, g_fin, "const", writes=[b_const])
    sch.op("dve", lambda e: e.memset(epsc, EPS), writes=[b_const])
    sch.barrier()

    def xsrc(l):
        return xT_in if l == 0 else XS

    def load_x_tile(dst, src, it):
        return src.rearrange("(c p) t -> p c t", p=128)[:, :, it * 512:(it + 1) * 512]

    def emit_rstd(ps_buf, ps_ap, rstd_buf, rstd_ap, tmp_ap, n, tmp_buf):
        sch.op("act", lambda e: e.activation(out=tmp_ap, in_=ps_ap, func=AF.Sqrt, scale=1.0 / n, bias=epsc),
               reads=[ps_buf, b_const], writes=[tmp_buf])
        return sch.op("dve", lambda e: e.reciprocal(out=rstd_ap, in_=tmp_ap), reads=[tmp_buf], writes=[rstd_buf])

    def phase_A(l):
        ar = Arena(CONST_END)
        wA = ar.alloc(BF16, 8, WIN_COLS)
        gm = ar.alloc(F32, 8)
        xt = [ar.alloc(F32, 8, 512) for _ in range(2)]
        sq = [ar.alloc(BF16, 8, 512) for _ in range(2)]
        hT = [ar.alloc(BF16, 8, 512) for _ in range(2)]
        rstd = [ar.alloc(F32, 512) for _ in range(2)]
        rtmp = [ar.alloc(F32, 512) for _ in range(2)]
        stg = [ar.alloc(BF16, NQK, 512) for _ in range(2)]
        vst = [ar.alloc(BF16, 4, 10, 65) for _ in range(2)]
        b_w = sch.buf("wA")
        b_gm = sch.buf("gm")
        b_xt = [sch.buf("xt0"), sch.buf("xt1")]
        b_sq = [sch.buf("sq0"), sch.buf("sq1")]
        b_hT = [sch.buf("hT0"), sch.buf("hT1")]
        b_rstd = [sch.buf("rstd0"), sch.buf("rstd1")]
        b_rtmp = [sch.buf("rtmp0"), sch.buf("rtmp1")]
        b_stg = [sch.buf("stg0"), sch.buf("stg1")]
        b_vst = [sch.buf("vst0"), sch.buf("vst1")]
        b_ps = [sch.buf(f"psA{i}") for i in range(8)]

        src = xsrc(l)
        NT = S // 512
        dma("sp", xt[0], load_x_tile(None, src, 0), "xt0", writes=[b_xt[0]])
        dma("sp", gm, g_mix[l], "gm", writes=[b_gm])
        WBLK = [(0, 512), (512, 1024), (1024, 1792), (1792, WIN_COLS)]
        b_wb = [sch.buf(f"wA{k}") for k in range(4)]
        for k, (c0_, c1_) in enumerate(WBLK):
            for c in range(8):
                dma("pool", wA[:, c, c0_:c1_], w_in[l, :, c, c0_:c1_], f"wA{k}", writes=[b_wb[k]])
        for s in range(2):
            sch.op("dve", lambda e, s=s: e.memset(vst[s][:, :, :, 64:65], 1.0), writes=[b_vst[s]])

        def norm(it):
            s = it % 2
            sch.op("act", lambda e, s=s: e.activation(out=sq[s], in_=xt[s], func=AF.Square),
                   reads=[b_xt[s]], writes=[b_sq[s]])

            def f_ssq(e, s=s):
                ins = None
                for c in range(8):
                    ins = e.matmul(PS[7][:, :], lhsT=ones, rhs=sq[s][:, c, :], start=(c == 0), stop=(c == 7))
                return ins
            sch.op("pe", f_ssq, reads=[b_sq[s], b_const], writes=[b_ps[7]])
            emit_rstd(b_ps[7], PS[7][:, :], b_rstd[s], rstd[s], rtmp[s], D, b_rtmp[s])

            def f_h(e, s=s):
                ins = None
                for c in range(8):
                    ins = e.scalar_tensor_tensor(out=hT[s][:, c, :], in0=xt[s][:, c, :], scalar=gm[:, c:c + 1],
                                                 in1=rstd[s], op0=ALU.mult, op1=ALU.mult)
                return ins
            sch.op("dve", f_h, reads=[b_xt[s], b_rstd[s], b_gm], writes=[b_hT[s]])

        def qk_chunk(it, j):
            s = it % 2
            pb = j % 4

            def f_mm(e, j=j, pb=pb, s=s):
                ins = None
                for c in range(8):
                    ins = e.matmul(PS[pb][:, :], lhsT=wA[:, c, j * 128:(j + 1) * 128], rhs=hT[s][:, c, :],
                                   start=(c == 0), stop=(c == 7))
                return ins
            sch.op("pe", f_mm, reads=[b_hT[s], b_wb[0 if j < 4 else (1 if j < 8 else 2)]], writes=[b_ps[pb]])
            is_q = j < 4 or 8 <= j < 12
            if is_q:
                sch.op("dve", lambda e, j=j, pb=pb, s=s: e.tensor_scalar(
                    out=stg[s][:, j, :], in0=PS[pb][:, :], scalar1=qscale[:, j:j + 1], scalar2=None,
                    op0=ALU.mult), reads=[b_ps[pb], b_const], writes=[b_stg[s]])
            else:
                sch.op("act", lambda e, j=j, pb=pb, s=s: e.activation(
                    out=stg[s][:, j, :], in_=PS[pb][:, :], func=AF.Copy),
                    reads=[b_ps[pb]], writes=[b_stg[s]])

        norm(0)
        for it in range(NT):
            s = it % 2
            if it + 1 < NT:
                dma("sp", xt[1 - s], load_x_tile(None, src, it + 1), f"xt{1 - s}", writes=[b_xt[1 - s]])
            for j in range(7):
                qk_chunk(it, j)
            if it + 1 < NT:
                norm(it + 1)
            for j in range(7, NQK):
                qk_chunk(it, j)
            dma("pool", QKT.rearrange("(j p) t -> p j t", p=128)[:, :, it * 512:(it + 1) * 512], stg[s],
                f"stgst{s}", reads=[b_stg[s]])
            for sub in range(4):
                pa = 4 + (sub % 2)

                def f_v(e, sub=sub, pa=pa, s=s):
                    ins = None
                    for c in range(8):
                        ins = e.matmul(PS[pa][:, :], lhsT=hT[s][:, c, sub * 128:(sub + 1) * 128],
                                       rhs=wA[:, c, 1792:2304], start=(c == 0), stop=(c == 7))
                    return ins
                sch.op("pe", f_v, reads=[b_hT[s], b_wb[3]], writes=[b_ps[pa]])
                sch.op("act", lambda e, sub=sub, pa=pa, s=s: e.activation(
                    out=vst[s][:, sub, 0:8, 0:64], in_=PS[pa][:, :].rearrange("p (h d) -> p h d", h=8),
                    func=AF.Copy), reads=[b_ps[pa]], writes=[b_vst[s]])

                def f_v2(e, sub=sub, s=s):
                    ins = None
                    for c in range(8):
                        ins = e.matmul(PS[6][:, 0:128], lhsT=hT[s][:, c, sub * 128:(sub + 1) * 128],
                                       rhs=wA[:, c, 2304:2432], start=(c == 0), stop=(c == 7))
                    return ins
                sch.op("pe", f_v2, reads=[b_hT[s], b_wb[3]], writes=[b_ps[6]])
                sch.op("dve", lambda e, sub=sub, s=s: e.tensor_copy(
                    out=vst[s][:, sub, 8:10, 0:64], in_=PS[6][:, 0:128].rearrange("p (h d) -> p h d", h=2)),
                    reads=[b_ps[6]], writes=[b_vst[s]])
            dma("pool", VS[it * 512:(it + 1) * 512, :].rearrange("(u p) w -> p u w", p=128),
                vst[s].rearrange("p u h d -> p u (h d)"), f"vstst{s}", reads=[b_vst[s]])
        sch.barrier()

    WG_OFF, WU_OFF, WD_OFF = 76800, 121856, 166912
    assert WD_OFF + NF * D * 2 <= NB16 * 2

    def at(off, dt, *free):
        a_ = Arena(off)
        return a_.alloc(dt, *free)

    wg = at(WG_OFF, BF16, 8, DFF)
    wu = at(WU_OFF, BF16, 8, DFF)
    wd = at(WD_OFF, BF16, NF, D)
    b_wC = sch.buf("wC")
    LOOK = 2

    def prefetch_list(l):
        lst = [(wu[:, c, :], w_up[l, :, c, :]) for c in range(8)]
        lst += [(wd[:, f, :], w_down[l, :, f, :]) for f in range(NF)]
        return lst

    def prefetch_C(l, which):
        if which == "g":
            for c in range(2, 8):
                dma("pool", wg[:, c, :], w_gate[l, :, c, :], "wC", writes=[b_wC])
        if which == "g01":
            for c in range(2):
                dma("pool", wg[:, c, :], w_gate[l, :, c, :], "wC", writes=[b_wC])

    def phase_Bw(l):
        ar = Arena(CONST_END)
        QA = ar.alloc(BF16, 4, S)
        KA = ar.alloc(BF16, 4, S)
        QB = ar.alloc(BF16, 4, S)
        KB = ar.alloc(BF16, 2, S)
        VB = ar.alloc(BF16, 32, 130)
        pTw = [ar.alloc(BF16, 384) for _ in range(6)]
        ostw = [ar.alloc(F32, 520) for _ in range(2)]
        b_qk = sch.buf("qkB")
        b_qkA = sch.buf("qkA")
        b_VB = sch.buf("VB")
        b_pTw = [sch.buf(f"pTw{i}") for i in range(3)]
        b_Sw = [sch.buf(f"Sw{i}") for i in range(3)]
        b_oB = [sch.buf(f"oB{i}") for i in range(2)]
        b_ostw = [sch.buf(f"ostw{i}") for i in range(2)]
        qkv = QKT.rearrange("(j p) t -> p j t", p=128)
        for j in range(4):
            dma("sp", QB[:, j, :], qkv[:, 8 + j, :], "qkB", writes=[b_qk])
        for j in range(2):
            dma("sp", KB[:, j, :], qkv[:, 12 + j, :], "qkB", writes=[b_qk])
        for q4 in range(4):
            src = bass.AP(VS.tensor, VS.offset + q4 * 1024 * VW + 520, [[VW, 128], [VW * 128, 8], [1, 130]])
            dma("sp", VB[:, q4 * 8:(q4 + 1) * 8, :], src, "VB", writes=[b_VB])
        for j in range(4):
            dma("sp", QA[:, j, :], qkv[:, j, :], "qkA", writes=[b_qkA])
            dma("sp", KA[:, j, :], qkv[:, 4 + j, :], "qkA", writes=[b_qkA])

        NTT = S // 128
        items = [(i, c) for i in range(NTT) for c in range(4)]
        N = len(items)
        LK = 2

        def jbs_of(i):
            return [jb for jb in range(3) if 0 <= i - 1 + jb < NTT]

        for k in range(N + LK):
            if k < N:
                i, c = items[k]
                g = c // 2
                slot = k % 3
                jbs = jbs_of(i)
                c0, c1 = jbs[0] * 128, jbs[-1] * 128 + 128

                def f_s(e, i=i, c=c, g=g, slot=slot, jbs=jbs, c0=c0, c1=c1):
                    ins = None
                    for hh in range(2):
                        e.matmul(PS[2 * slot + hh][:, c0:c1], lhsT=ident, rhs=dwt[:, c0:c1], start=True, stop=False)
                    for jb in jbs:
                        j = i - 1 + jb
                        for hh in range(2):
                            ksel = 0 if g == hh else 1
                            kk = sub_ap(KB, 64 * hh, 64, ksel * S + j * 128, [[1, 128]])
                            qv = sub_ap(QB, 64 * hh, 64, c * S + i * 128, [[1, 128]])
                            ins = e.matmul(PS[2 * slot + hh][:, jb * 128:(jb + 1) * 128], lhsT=kk, rhs=qv,
                                           start=False, stop=(jb == jbs[-1]))
                    return ins
                sch.op("pe", f_s, reads=[b_qk, b_const], writes=[b_Sw[slot]])

                def f_e(e, c=c, slot=slot, c0=c0, c1=c1):
                    ins = None
                    for hh in range(2):
                        ins = e.activation(out=pTw[2 * slot + hh][:, c0:c1], in_=PS[2 * slot + hh][:, c0:c1],
                                           func=AF.Exp, scale=float(SL_WIN[2 * c + hh]))
                    return ins
                sch.op("act", f_e, reads=[b_Sw[slot]], writes=[b_pTw[slot]])
            kk_ = k - LK
            if kk_ >= 0:
                i, c = items[kk_]
                g = c // 2
                slot = kk_ % 3
                jbs = jbs_of(i)
                ob = c // 2

                def f_pv(e, slot=slot, jbs=jbs, i=i, g=g, c=c, ob=ob):
                    ins = None
                    for hh in range(2):
                        h = 2 * c + hh
                        ops_ = PS[6 + ob][:, (h % 4) * 65:(h % 4) * 65 + 65]
                        for n_, jb in enumerate(jbs):
                            j = i - 1 + jb
                            ins = e.matmul(ops_, lhsT=pTw[2 * slot + hh][:, jb * 128:(jb + 1) * 128],
                                           rhs=VB[:, j, g * 65:(g + 1) * 65], start=(n_ == 0),
                                           stop=(n_ == len(jbs) - 1))
                    return ins
                sch.op("pe", f_pv, reads=[b_pTw[slot], b_VB], writes=[b_oB[ob]])
                if c == 3:
                    os_ = i % 2
                    ob0 = 0

                    def f_ev(e, os_=os_, ob0=ob0):
                        e.tensor_copy(out=ostw[os_][:, 0:260], in_=PS[6][:, 0:260])
                        return e.tensor_copy(out=ostw[os_][:, 260:520], in_=PS[7][:, 0:260])
                    sch.op("dve", f_ev, reads=[b_oB[ob0], b_oB[ob0 + 1]], writes=[b_ostw[os_]])
                    dma("pool", OP[3, i * 128:(i + 1) * 128, :], ostw[os_], f"ostw{os_}", reads=[b_ostw[os_]])
        sch.barrier()
        return QA, KA

    def phase_Bd(l, QA, KA):
        pf = prefetch_list(l)
        ar = Arena(CONST_END + 2 * 4 * S * 2)
        vown = [ar.alloc(BF16, 8, 65) for _ in range(3)]
        vcmp = [ar.alloc(BF16, 8, 65) for _ in range(3)]
        pT = [ar.alloc(BF16, 512) for _ in range(4)]
        KC = [ar.alloc(BF16, 4, 128) for _ in range(3)]
        ost = [ar.alloc(F32, 520) for _ in range(2)]
        ddt = ar.alloc(BF16, 9, 512)
        LOOK = 2
        assert ar.off <= WU_OFF
        b_ddt = sch.buf("ddt")
        dma("sp", ddt, ddt_d.rearrange("a p c -> p a c"), "ddt", writes=[b_ddt])
        b_qk = sch.buf("qkA2")
        b_kc = [sch.buf(f"kc{i}") for i in range(3)]
        b_vown = [sch.buf(f"vown{i}") for i in range(3)]
        b_vcmp = [sch.buf(f"vcmp{i}") for i in range(3)]
        b_pT = [sch.buf(f"pT{i}") for i in range(4)]
        b_ost = [sch.buf(f"ost{i}") for i in range(2)]
        b_S = [sch.buf(f"S{i}") for i in range(4)]
        b_o = [sch.buf(f"o{i}") for i in range(4)]

        tiles = []
        for p, dil in enumerate(DILS):
            L = S // dil
            for r in range(dil):
                for m in range(L // 128):
                    tiles.append((p, dil, L, r, m))
        if b1_limit is not None:
            tiles = [tiles[k] for k in b1_limit]

        def geom(ti):
            p, dil, L, r, m = tiles[ti]
            nm = L // 128
            la = 128 * m - 64 if m > 0 else 0
            lb = 128 * m + 128 if m < nm - 1 else 128 * m
            var = 1 if m == 0 else (2 if m == nm - 1 else 0)
            t0 = r + dil * 128 * m
            return p, dil, r, m, la, lb, var, t0

        def v_loads(ti):
            slot = ti % 3
            p, dil, r, m, la, lb, var, t0 = geom(ti)
            src = bass.AP(VS.tensor, VS.offset + t0 * VW, [[dil * VW, 128], [1, 520]])
            dma("sp", vown[slot].rearrange("p h d -> p (h d)"), src, f"vown{slot}", writes=[b_vown[slot]])
            for half, l0 in ((0, la), (1, lb)):
                src = bass.AP(VS.tensor, VS.offset + (r + dil * l0) * VW, [[dil * VW, 64], [1, 520]])
                dma("sp", vcmp[slot][64 * half:64 * half + 64].rearrange("p h d -> p (h d)"), src,
                    f"vcmp{slot}", writes=[b_vcmp[slot]])

            def f_kc(e, slot=slot, r=r, dil=dil, la=la, lb=lb):
                ins = None
                for half, l0 in ((0, la), (1, lb)):
                    ins = e.tensor_copy(out=KC[slot][:, :, 64 * half:64 * half + 64],
                                        in_=sub_ap(KA, 0, 128, r + dil * l0, [[S, 4], [dil, 64]]))
                return ins
            sch.op("dve", f_kc, reads=[b_qk], writes=[b_kc[slot]])

        for ti in range(min(3, len(tiles))):
            v_loads(ti)
        items = [(ti, c) for ti in range(len(tiles)) for c in range(4)]
        N = len(items)
        LOOK = 2
        for k in range(N + LOOK):
            if k < N:
                ti, c = items[k]
                p, dil, r, m, la, lb, var, t0 = geom(ti)
                slot = ti % 3
                sslot = k % 3

                def f_s(e, sslot=sslot, slot=slot, c=c, p=p, var=var, t0=t0, dil=dil):
                    for hh in range(2):
                        e.matmul(PS[2 * sslot + hh][:, 0:256], lhsT=ident, rhs=ddt[:, p * 3 + var, 0:256],
                                 start=True, stop=False)
                    ins = None
                    for part in range(2):
                        for hh in range(2):
                            qv = sub_ap(QA, 64 * hh, 64, c * S + t0, [[dil, 128]])
                            if part == 0:
                                kx = sub_ap(KA, 64 * hh, 64, c * S + t0, [[dil, 128]])
                            else:
                                kx = KC[slot][64 * hh:64 * hh + 64, c, :]
                            ins = e.matmul(PS[2 * sslot + hh][:, part * 128:part * 128 + 128], lhsT=kx, rhs=qv,
                                           start=False, stop=(part == 1))
                    return ins
                sch.op("pe", f_s, reads=[b_qk, b_const, b_kc[slot], b_ddt], writes=[b_S[sslot]])

                def f_e(e, sslot=sslot, c=c):
                    ins = None
                    for hh in range(2):
                        ins = e.activation(out=pT[sslot][:, hh * 256:(hh + 1) * 256],
                                           in_=PS[2 * sslot + hh][:, 0:256], func=AF.Exp,
                                           scale=float(SL_DIL[2 * c + hh]))
                    return ins
                sch.op("act", f_e, reads=[b_S[sslot]], writes=[b_pT[sslot]])
            kk_ = k - LOOK
            if kk_ >= 0:
                ti, c = items[kk_]
                p, dil, r, m, la, lb, var, t0 = geom(ti)
                slot = ti % 3
                pslot = kk_ % 3
                ob = 0
                obank = 6 + c // 2

                def f_pv(e, pslot=pslot, slot=slot, c=c, obank=obank):
                    ins = None
                    for hh in range(2):
                        h = 2 * c + hh
                        ops_ = PS[obank][:, (h % 4) * 65:(h % 4) * 65 + 65]
                        e.matmul(ops_, lhsT=pT[pslot][:, hh * 256:hh * 256 + 128], rhs=vown[slot][:, h, :],
                                 start=True, stop=False)
                        ins = e.matmul(ops_, lhsT=pT[pslot][:, hh * 256 + 128:hh * 256 + 256],
                                       rhs=vcmp[slot][:, h, :], start=False, stop=True)
                    return ins
                sch.op("pe", f_pv, reads=[b_pT[pslot], b_vown[slot], b_vcmp[slot]], writes=[b_o[ob + c // 2]])
                if c == 3:
                    os_ = ti % 2

                    def f_ev(e, os_=os_, ob=ob):
                        e.tensor_copy(out=ost[os_][:, 0:260], in_=PS[6][:, 0:260])
                        return e.tensor_copy(out=ost[os_][:, 260:520], in_=PS[7][:, 0:260])
                    sch.op("dve", f_ev, reads=[b_o[ob], b_o[ob + 1]], writes=[b_ost[os_]])
                    dst = bass.AP(OP.tensor, OP.offset + p * S * 520 + t0 * 520, [[dil * 520, 128], [1, 520]])
                    dma("sp", dst, ost[os_], f"ostst{os_}", reads=[b_ost[os_]])
                    if ti + 3 < len(tiles):
                        v_loads(ti + 3)
                    if pf:
                        o_, i_ = pf.pop(0)
                        dma("pool", o_, i_, "wC", writes=[b_wC])
        while pf:
            o_, i_ = pf.pop(0)
            dma("pool", o_, i_, "wC", writes=[b_wC])
        sch.barrier()

    def phase_Be(l):
        ar = Arena(CONST_END)
        wo = ar.alloc(BF16, 8, D)
        gob = ar.alloc(F32, D)
        esink = ar.alloc(F32, 8)
        NOS = 3
        osl = [[ar.alloc(F32, 520) for _ in range(4)] for _ in range(NOS)]
        ya = [ar.alloc(F32, 512) for _ in range(2)]
        yb = [ar.alloc(F32, 512) for _ in range(2)]
        sm = [ar.alloc(F32, 32) for _ in range(2)]
        yn = [ar.alloc(BF16, D) for _ in range(2)]
        ynT = ar.alloc(BF16, 8, 512)
        xtb = ar.alloc(F32, 8, 512)
        assert ar.off <= WG_OFF + 2 * DFF * 2, ar.off
        b_wo = sch.buf("wo")
        b_small = sch.buf("smallB")
        b_osl = [sch.buf(f"osl{i}") for i in range(NOS)]
        b_ep = [sch.buf("ep0"), sch.buf("ep1")]
        b_yn = [sch.buf("yn0"), sch.buf("yn1")]
        b_ynT = sch.buf("ynT")
        b_xtb = sch.buf("xtB")
        b_tp = [sch.buf("tp0"), sch.buf("tp1")]
        b_op = [sch.buf("opj0"), sch.buf("opj1")]
        for c in range(8):
            dma("pool", wo[:, c, :], w_out[l, :, c, :], "wo", writes=[b_wo])
        prefetch_C(l, "g")
        dma("sp", gob, g_out[l], "smallB", writes=[b_small])
        dma("sp", esink, sinkb[l], "smallB", writes=[b_small])
        sch.op("act", lambda e: e.activation(out=esink, in_=esink, func=AF.Exp), reads=[b_small], writes=[b_small])

        def o_loads(i):
            s = i % NOS
            for p in range(4):
                dma("sp", osl[s][p], OP[p, i * 128:(i + 1) * 128, :], f"osl{s}", writes=[b_osl[s]])

        def bc(ap8, n, stride=1):
            return bass.AP(ap8.tensor, ap8.offset, [list(ap8.ap[0]), [stride, n], [0, 64]])

        src = xsrc(l)
        NTT = S // 128

        def stage_X(i):
            s = i % 2
            so = i % NOS
            o0, o1, o2, o3 = osl[so]
            o0v = o0.rearrange("p (h d) -> p h d", h=8)
            o3v = o3.rearrange("p (h d) -> p h d", h=8)
            sm_, ya_, yb_, yn_ = sm[s], ya[s], yb[s], yn[s]

            def f_1(e):
                e.memset(sm_[:, 24:26], 0.0)
                e.tensor_tensor(out=sm_[:, 8:16], in0=o3v[:, :, 64], in1=esink, op=ALU.add)
                return e.tensor_tensor(out=o0, in0=o0, in1=o1, op=ALU.add)
            sch.op("dve", f_1, reads=[b_osl[so], b_small], writes=[b_osl[so], b_ep[s]])

            def f_2(e):
                e.reciprocal(out=sm_[:, 16:24], in_=sm_[:, 8:16])
                return e.tensor_tensor(out=o0, in0=o0, in1=o2, op=ALU.add)
            sch.op("dve", f_2, reads=[b_osl[so], b_ep[s]], writes=[b_osl[so], b_ep[s]])

            def f_3(e):
                e.reciprocal(out=sm_[:, 0:8], in_=o0v[:, :, 64])
                return e.tensor_tensor(out=yb_.rearrange("p (h d) -> p h d", h=8), in0=o3v[:, :, 0:64],
                                       in1=bc(sm_[:, 16:24], 8), op=ALU.mult)
            sch.op("dve", f_3, reads=[b_osl[so], b_ep[s]], writes=[b_ep[s]])
            sch.op("dve", lambda e: e.tensor_tensor(
                out=ya_.rearrange("p (h d) -> p h d", h=8), in0=o0v[:, :, 0:64], in1=bc(sm_[:, 0:8], 8),
                op=ALU.mult), reads=[b_osl[so], b_ep[s]], writes=[b_ep[s]])
            if i + NOS < NTT:
                o_loads(i + NOS)

            def f_ss(e):
                e.activation(out=yn_[:, 0:512], in_=ya_, func=AF.Square, accum_out=sm_[:, 24:25])
                return e.activation(out=yn_[:, 512:1024], in_=yb_, func=AF.Square, accum_out=sm_[:, 25:26])
            sch.op("act", f_ss, reads=[b_ep[s]], writes=[b_ep[s], b_yn[s]])
            sch.op("act", lambda e: e.activation(out=sm_[:, 26:28], in_=sm_[:, 24:26], func=AF.Sqrt, scale=1.0 / 512,
                                                 bias=epsc), reads=[b_ep[s], b_const], writes=[b_ep[s]])

        def stage_Y(i):
            s = i % 2
            big = i // 4
            sub = i % 4
            sm_, ya_, yb_, yn_ = sm[s], ya[s], yb[s], yn[s]
            if sub == 0:
                dma("sp", xtb, load_x_tile(None, src, big), "xtB0", writes=[b_xtb])

            sch.op("dve", lambda e: e.reciprocal(out=sm_[:, 28:30], in_=sm_[:, 26:28]),
                   reads=[b_ep[s]], writes=[b_ep[s]])

            def f_4(e):
                e.scalar_tensor_tensor(out=yn_[:, 0:512], in0=ya_, scalar=sm_[:, 28:29], in1=gob[:, 0:512],
                                       op0=ALU.mult, op1=ALU.mult)
                return e.scalar_tensor_tensor(out=yn_[:, 512:1024], in0=yb_, scalar=sm_[:, 29:30],
                                              in1=gob[:, 512:1024], op0=ALU.mult, op1=ALU.mult)
            sch.op("dve", f_4, reads=[b_ep[s], b_small], writes=[b_yn[s]])
            if debug and dump:
                dma("pool", DBG_YN[i * 128:(i + 1) * 128, :], yn_, f"dbg0{s}", reads=[b_yn[s]])
                dma("pool", DBG_Y[i * 128:(i + 1) * 128, 0:512], ya_, f"dbg1{s}", reads=[b_ep[s]])
                dma("pool", DBG_Y[i * 128:(i + 1) * 128, 512:1024], yb_, f"dbg2{s}", reads=[b_ep[s]])
            tb = i % 2

            def f_tp(e, tb=tb):
                ins = None
                for c in range(8):
                    ins = e.transpose(out=PSB[6 + tb][:, c * 128:(c + 1) * 128], in_=yn_[:, c * 128:(c + 1) * 128],
                                      identity=ident)
                return ins
            sch.op("pe", f_tp, reads=[b_yn[s], b_const], writes=[b_tp[tb]])
            sch.op("act", lambda e, sub=sub, tb=tb: e.activation(
                out=ynT[:, :, sub * 128:(sub + 1) * 128],
                in_=PSB[6 + tb][:, 0:1024].rearrange("p (c t) -> p c t", c=8), func=AF.Copy),
                reads=[b_tp[tb]], writes=[b_ynT])
            if sub == 3:
                for d in range(8):
                    pb = d % 2

                    def f_o(e, d=d, pb=pb):
                        ins = None
                        for c in range(8):
                            ins = e.matmul(PS[4 + pb][:, :], lhsT=wo[:, c, d * 128:(d + 1) * 128], rhs=ynT[:, c, :],
                                           start=(c == 0), stop=(c == 7))
                        return ins
                    sch.op("pe", f_o, reads=[b_ynT, b_wo], writes=[b_op[pb]])
                    sch.op("dve", lambda e, d=d, pb=pb: e.tensor_tensor(
                        out=xtb[:, d, :], in0=PS[4 + pb][:, :], in1=xtb[:, d, :], op=ALU.add),
                        reads=[b_op[pb], b_xtb], writes=[b_xtb])
                dma("pool", XS.rearrange("(c p) t -> p c t", p=128)[:, :, big * 512:(big + 1) * 512], xtb,
                    "xtBst0", reads=[b_xtb])

        for i0 in range(NOS):
            o_loads(i0)
        stage_X(0)
        for i in range(NTT):
            if i + 1 < NTT:
                stage_X(i + 1)
            stage_Y(i)
        sch.barrier()

    def phase_B(l):
        QA, KA = phase_Bw(l)
        if stop_after == ("Bw", l):
            return
        phase_Bd(l, QA, KA)
        if stop_after == ("B1", l):
            return
        phase_Be(l)

    def phase_C(l):
        last = (l == DEPTH - 1)
        ar = Arena(CONST_END)
        gm = ar.alloc(F32, 8)
        xt = [ar.alloc(F32, 8, 512) for _ in range(2)]
        hT = ar.alloc(BF16, 8, 512)
        ar.off = (ar.off + 63) // 64 * 64
        aoff = ar.off
        actT = ar.alloc(BF16, NF, 512)
        sqf = actT[:, 0:8, :]
        ostf = [SBF[:, aoff // 4 + 2048 + k * 1024: aoff // 4 + 2048 + (k + 1) * 1024] for k in range(2)]
        rstd = ar.alloc(F32, 512)
        sg = [ar.alloc(F32, 512) for _ in range(2)]
        if last:
            gfb = ar.alloc(F32, D)
            smf = [ar.alloc(F32, 8) for _ in range(2)]
        assert ar.off <= WG_OFF, ar.off
        b_w = b_wC
        b_gm = sch.buf("gmC")
        b_xt = [sch.buf("xtC0"), sch.buf("xtC1")]
        b_hT = sch.buf("hTC")
        b_act = sch.buf("actT")
        b_rstd = sch.buf("rstdC")
        b_gfb = sch.buf("gfb")
        b_ostf = [sch.buf("ostf0"), sch.buf("ostf1")]
        b_smf = [sch.buf("smf0"), sch.buf("smf1")]
        if last:
            dma("sp", gfb, g_finb, "gfb", writes=[b_gfb])
        b_sg = [sch.buf("sg0"), sch.buf("sg1")]
        b_ps = [sch.buf(f"psC{i}") for i in range(8)]
        prefetch_C(l, "g01")
        dma("sp", gm, g_ffn[l], "gmC", writes=[b_gm])
        NT = S // 512
        dma("sp", xt[0], load_x_tile(None, XS, 0), "xtC0", writes=[b_xt[0]])

        def norm_sq(it):
            s = it % 2
            sch.op("act", lambda e, s=s: e.activation(out=hT, in_=xt[s], func=AF.Square),
                   reads=[b_xt[s]], writes=[b_hT])

        def norm_rest(it):
            s = it % 2

            def f_ssq(e):
                ins = None
                for c in range(8):
                    ins = e.matmul(PS[7][:, :], lhsT=ones, rhs=hT[:, c, :], start=(c == 0), stop=(c == 7))
                return ins
            sch.op("pe", f_ssq, reads=[b_hT, b_const], writes=[b_ps[7]])
            emit_rstd(b_ps[7], PS[7][:, :], b_rstd, rstd, sg[0], D, tmp_buf=b_sg[0])

            def f_h(e, s=s):
                ins = None
                for c in range(8):
                    ins = e.scalar_tensor_tensor(out=hT[:, c, :], in0=xt[s][:, c, :], scalar=gm[:, c:c + 1],
                                                 in1=rstd, op0=ALU.mult, op1=ALU.mult)
                return ins
            sch.op("dve", f_h, reads=[b_xt[s], b_rstd, b_gm], writes=[b_hT])

        def down(it, d):
            s = it % 2
            pd = 4 + d % 2

            def f_d(e, d=d, pd=pd):
                ins = None
                for f in range(NF):
                    ins = e.matmul(PS[pd][:, :], lhsT=wd[:, f, d * 128:(d + 1) * 128], rhs=actT[:, f, :],
                                   start=(f == 0), stop=(f == NF - 1))
                return ins
            sch.op("pe", f_d, reads=[b_act, b_w], writes=[b_ps[pd]])
            sch.op("dve", lambda e, d=d, pd=pd, s=s: e.tensor_tensor(
                out=xt[s][:, d, :], in0=PS[pd][:, :], in1=xt[s][:, d, :], op=ALU.add),
                reads=[b_ps[pd], b_xt[s]], writes=[b_xt[s]])

        norm_sq(0)
        norm_rest(0)
        for it in range(NT):
            s = it % 2
            if it + 1 < NT:
                dma("sp", xt[1 - s], load_x_tile(None, XS, it + 1), f"xtC{1 - s}", writes=[b_xt[1 - s]])
            for f in range(NF):
                pg = (f % 2) * 2

                def f_g(e, f=f, pg=pg):
                    for c in range(8):
                        e.matmul(PS[pg][:, :], lhsT=wg[:, c, f * 128:(f + 1) * 128], rhs=hT[:, c, :],
                                 start=(c == 0), stop=(c == 7))
                    ins = None
                    for c in range(8):
                        ins = e.matmul(PS[pg + 1][:, :], lhsT=wu[:, c, f * 128:(f + 1) * 128], rhs=hT[:, c, :],
                                       start=(c == 0), stop=(c == 7))
                    return ins
                sch.op("pe", f_g, reads=[b_hT, b_w], writes=[b_ps[pg], b_ps[pg + 1]])
                sch.op("act", lambda e, f=f, pg=pg: e.activation(out=sg[f % 2], in_=PS[pg][:, :], func=AF.Silu),
                       reads=[b_ps[pg]], writes=[b_sg[f % 2]])
                sch.op("dve", lambda e, f=f, pg=pg: e.tensor_tensor(
                    out=actT[:, f, :], in0=PS[pg + 1][:, :], in1=sg[f % 2], op=ALU.mult),
                    reads=[b_sg[f % 2], b_ps[pg + 1]],
                    writes=[b_act] + (b_ostf if (last and 8 <= f < 16) else []))
            if it + 1 < NT:
                norm_sq(it + 1)
            for d in range(4):
                down(it, d)
            if it + 1 < NT:
                norm_rest(it + 1)
            for d in range(4, 8):
                down(it, d)
            if not last:
                dma("pool", XS.rearrange("(c p) t -> p c t", p=128)[:, :, it * 512:(it + 1) * 512], xt[s],
                    f"xtCst{s}", reads=[b_xt[s]])
            else:
                for sub in range(4):
                    osb = sub % 2
                    ostg = ostf[osb]
                    smx = smf[osb]
                    bk = [4 + 2 * (sub % 2), 5 + 2 * (sub % 2)]

                    def f_t(e, sub=sub, bk=bk, s=s):
                        ins = None
                        for c in range(8):
                            ins = e.transpose(out=PS[bk[c // 4]][:, (c % 4) * 128:(c % 4 + 1) * 128],
                                              in_=xt[s][:, c, sub * 128:(sub + 1) * 128], identity=identf)
                        return ins
                    sch.op("pe", f_t, reads=[b_xt[s], b_const], writes=[b_ps[bk[0]], b_ps[bk[1]]])
                    sch.op("dve", lambda e, smx=smx: e.memset(smx[:, 0:2], 0.0), writes=[b_smf[osb]])

                    def f_q(e, bk=bk, ostg=ostg, smx=smx):
                        e.activation(out=ostg[:, 0:512], in_=PS[bk[0]][:, :], func=AF.Square, accum_out=smx[:, 0:1])
                        return e.activation(out=ostg[:, 512:1024], in_=PS[bk[1]][:, :], func=AF.Square,
                                            accum_out=smx[:, 1:2])
                    sch.op("act", f_q, reads=[b_ps[bk[0]], b_ps[bk[1]]], writes=[b_smf[osb], b_ostf[osb]])
                    sch.op("dve", lambda e, smx=smx: e.tensor_tensor(out=smx[:, 2:3], in0=smx[:, 0:1], in1=smx[:, 1:2],
                                                                    op=ALU.add),
                           reads=[b_smf[osb]], writes=[b_smf[osb]])
                    sch.op("act", lambda e, smx=smx: e.activation(out=smx[:, 3:4], in_=smx[:, 2:3], func=AF.Sqrt,
                                                                  scale=1.0 / D, bias=epsc),
                           reads=[b_smf[osb], b_const], writes=[b_smf[osb]])
                    sch.op("dve", lambda e, smx=smx: e.reciprocal(out=smx[:, 4:5], in_=smx[:, 3:4]),
                           reads=[b_smf[osb]], writes=[b_smf[osb]])

                    def f_y(e, bk=bk, ostg=ostg, smx=smx):
                        e.scalar_tensor_tensor(out=ostg[:, 0:512], in0=PS[bk[0]][:, :], scalar=smx[:, 4:5],
                                               in1=gfb[:, 0:512], op0=ALU.mult, op1=ALU.mult)
                        return e.scalar_tensor_tensor(out=ostg[:, 512:1024], in0=PS[bk[1]][:, :], scalar=smx[:, 4:5],
                                                      in1=gfb[:, 512:1024], op0=ALU.mult, op1=ALU.mult)
                    sch.op("dve", f_y, reads=[b_ps[bk[0]], b_ps[bk[1]], b_smf[osb], b_gfb], writes=[b_ostf[osb]])
                    o = dma("pool", out_d[it * 512 + sub * 128: it * 512 + (sub + 1) * 128, :], ostg,
                            f"outst{osb}", reads=[b_ostf[osb]])
                    out_dmas.append(o)
        sch.barrier()

    out_dmas = []

    done = False
    for l in range(DEPTH):
        if done:
            break
        if "A" not in skip:
            phase_A(l)
        if stop_after == ("A", l):
            break
        phase_B(l)
        if stop_after in (("Bw", l), ("B1", l), ("B", l)):
            break
        phase_C(l)
        if stop_after == ("C", l):
            break

    sch.finalize_counts()
    eng_sems = {e: stack.enter_context(nc.semaphore(f"s_{e}")) for e in ENGS}
    dma_sems = {k: stack.enter_context(nc.semaphore(f"d_{k}")) for k in sch.dma_counts}
    handles = {"pe": "tensor", "act": "scalar", "dve": "vector", "pool": "gpsimd", "sp": "sync"}

    def replay(en, e):
        waited = {}

        def wait(sem, val):
            key = sem.name
            if waited.get(key, 0) < val:
                e.wait_ge(sem, val)
                waited[key] = val

        for o in sch.ops[en]:
            for d in o.deps:
                if d.dma_key is not None:
                    wait(dma_sems[d.dma_key], d.dma_count)
                else:
                    if d.fn is None:
                        continue
                    if d.eng == en:
                        if en == "pe" or not SAME_ENGINE_SYNC:
                            continue
                    wait(eng_sems[d.eng], d.count)
            if o.fn is None:
                continue
            ins = o.fn(e)
            if o.dma_key is not None:
                ins.then_inc(dma_sems[o.dma_key], 16)
            elif o.sig:
                ins.then_inc(eng_sems[en], 1)
        if en == "sp":
            for k, n in sch.dma_counts.items():
                wait(dma_sems[k], n)

    with nc.Block() as block:
        @block.tensor
        def _(e):
            replay("pe", e)

        @block.scalar
        def _(e):
            replay("act", e)

        @block.vector
        def _(e):
            replay("dve", e)

        @block.gpsimd
        def _(e):
            replay("pool", e)

        @block.sync
        def _(e):
            replay("sp", e)
    stack.close()
    return nc


def _chunked(w, nchunk):
    K, N = w.shape
    return np.ascontiguousarray(w.reshape(nchunk, 128, N).transpose(1, 0, 2))


def _vecT(g):
    return np.ascontiguousarray(g.reshape(-1, 128).T)


def _const_tables():
    bf = ml_dtypes.bfloat16
    ident = np.eye(128, dtype=np.float32)
    ones = np.ones((128, 128), np.float32)
    k = np.arange(128)[:, None].astype(np.float32)
    q = np.arange(128)[None, :].astype(np.float32)
    dd_own = np.zeros((3, 128, 128), np.float32)
    dd_comp = np.zeros((3, 3, 128, 128), np.float32)
    for p, dil in enumerate(DILS):
        dist = np.abs(k - q)
        dd_own[p] = np.where(dist <= 64, -dil * dist, -BIG)
        i = np.arange(128)[:, None].astype(np.float32)
        j = np.arange(128)[None, :].astype(np.float32)
        d_prev = 64 + j - i
        d_next = 64 + i - j
        full = np.where(i < 64, d_prev, d_next)
        base = np.where(full <= 64, -dil * full, -BIG)
        mid = base.copy()
        first = base.copy()
        first[:64] = -BIG
        lastv = base.copy()
        lastv[64:] = -BIG
        dd_comp[p, 0], dd_comp[p, 1], dd_comp[p, 2] = mid, first, lastv
    dw = np.zeros((128, 384), np.float32)
    dprev = 128 + q - k
    dw[:, 0:128] = np.where(dprev <= 128, -dprev, -BIG)
    dw[:, 128:256] = -np.abs(k - q)
    dnext = 128 + k - q
    dw[:, 256:384] = np.where(dnext <= 128, -dnext, -BIG)
    qs = np.ones((128, NQK), np.float32)
    for j in range(4):
        for p_ in range(128):
            qs[p_, j] = 1.0 / (8.0 * SL_DIL[2 * j + p_ // 64])
            qs[p_, 8 + j] = 1.0 / (8.0 * SL_WIN[2 * j + p_ // 64])
    ddt = np.zeros((9, 128, 512), np.float32)
    for p in range(3):
        for v in range(3):
            ddt[p * 3 + v] = np.concatenate([dd_own[p], dd_comp[p, v], dd_own[p], dd_comp[p, v]], axis=1)
    return dict(ident=ident.astype(bf), identf=ident, ones=ones.astype(bf), ddt=ddt.astype(bf),
                dw=dw.astype(bf), qscale=qs)


def _prep_shared(g_mix, w_in, g_out_dil, g_out_win, sink, w_out, g_ffn, w_gate, w_up, w_down, g_final):
    f = np.float32
    w_in = np.asarray(w_in, f)
    ext = np.concatenate([w_in[:, :, 0:512], w_in[:, :, 512:1024], w_in[:, :, 1536:2048], w_in[:, :, 2048:2176],
                          w_in[:, :, 2112:2176], w_in[:, :, 2048:2112], w_in[:, :, 1024:1536],
                          w_in[:, :, 2176:2304]], axis=2)
    sh = dict(
        w_in=np.stack([_chunked(ext[l], 8) for l in range(DEPTH)]),
        w_out=np.stack([_chunked(np.asarray(w_out[l], f), 8) for l in range(DEPTH)]),
        w_gate=np.stack([_chunked(np.asarray(w_gate[l], f), 8) for l in range(DEPTH)]),
        w_up=np.stack([_chunked(np.asarray(w_up[l], f), 8) for l in range(DEPTH)]),
        w_down=np.stack([_chunked(np.asarray(w_down[l], f), NF) for l in range(DEPTH)]),
        g_mix=np.stack([_vecT(np.asarray(g_mix[l], f)) for l in range(DEPTH)]),
        g_ffn=np.stack([_vecT(np.asarray(g_ffn[l], f)) for l in range(DEPTH)]),
        g_fin=_vecT(np.asarray(g_final, f)),
        g_finb=np.ascontiguousarray(np.broadcast_to(np.asarray(g_final, f)[None, :], (128, D))),
        g_out=np.stack([np.ascontiguousarray(np.broadcast_to(
            np.concatenate([np.asarray(g_out_dil[l], f), np.asarray(g_out_win[l], f)])[None, :], (128, D)))
            for l in range(DEPTH)]),
        sinkb=np.stack([np.ascontiguousarray(np.broadcast_to(np.asarray(sink[l], f)[None, :], (128, 8)))
                        for l in range(DEPTH)]),
    )
    sh.update(_const_tables())
    return sh


_NC_CACHE = {}


def kernel(x, g_mix, w_in, g_out_dil, g_out_win, sink, w_out, g_ffn, w_gate, w_up, w_down, g_final):
    x = np.asarray(x, np.float32)
    sh = _prep_shared(g_mix, w_in, g_out_dil, g_out_win, sink, w_out, g_ffn, w_gate, w_up, w_down, g_final)
    if "nc" not in _NC_CACHE:
        _NC_CACHE["nc"] = build_nc()
    nc = _NC_CACHE["nc"]
    in_maps = []
    for b in range(NCORES):
        m = dict(sh)
        m["xT"] = np.ascontiguousarray(x[b].T)
        in_maps.append(m)
    res = run_bass_kernel_spmd(nc, in_maps, core_ids=list(range(NCORES)))
    return np.stack([np.asarray(r["out"], np.float32) for r in res.results], axis=0)
```

```python
import math
from contextlib import ExitStack

import numpy as np
import ml_dtypes

import concourse.bass as bass
import concourse.mybir as mybir
from concourse.bass_utils import run_bass_kernel_spmd

F32 = mybir.dt.float32
BF16 = mybir.dt.bfloat16
AF = mybir.ActivationFunctionType
ALU = mybir.AluOpType

S = 4096
D = 1024
DEPTH = 2
DFF = 2816
NF = DFF // 128
EPS = 1e-6
NCORES = 8
WIN_COLS = 2432
NQK = 14
VW = 650
BIG = 131072.0
DILS = (1, 4, 16)
SLOPES = np.array([2.0 ** (-8.0 * (i + 1) / 16) for i in range(16)], dtype=np.float32)
SL_WIN = SLOPES[:8]
SL_DIL = SLOPES[8:]

SAME_ENGINE_SYNC = True


class Buf:
    __slots__ = ("name", "writer", "readers")

    def __init__(self, name):
        self.name = name
        self.writer = None
        self.readers = []


class Op:
    __slots__ = ("eng", "fn", "deps", "sig", "count", "dma_key", "dma_count")

    def __init__(self, eng, fn, dma_key):
        self.eng = eng
        self.fn = fn
        self.deps = []
        self.sig = False
        self.count = 0
        self.dma_key = dma_key
        self.dma_count = 0


ENGS = ("pe", "act", "dve", "pool", "sp")


class Sched:
    def __init__(self):
        self.ops = {e: [] for e in ENGS}
        self.dma_counts = {}
        self.all_bufs = []

    def buf(self, name):
        b = Buf(name)
        self.all_bufs.append(b)
        return b

    def op(self, eng, fn, reads=(), writes=(), dma_key=None, extra_deps=()):
        o = Op(eng, fn, dma_key)
        deps = []
        seen = set()

        def add(d):
            if d is not None and id(d) not in seen:
                seen.add(id(d))
                deps.append(d)

        for b in reads:
            add(b.writer)
        for b in writes:
            w = b.writer
            if not (w is not None and dma_key is not None and w.dma_key == dma_key and w.eng == eng):
                add(w)
            for r in b.readers:
                add(r)
        for d in extra_deps:
            add(d)
        for b in reads:
            b.readers.append(o)
        for b in writes:
            b.writer = o
            b.readers = []
        o.deps = deps
        for d in deps:
            d.sig = True
        if dma_key is not None:
            n = self.dma_counts.get(dma_key, 0) + 16
            self.dma_counts[dma_key] = n
            o.dma_count = n
        self.ops[eng].append(o)
        return o

    def barrier(self):
        lasts = []
        for e in ENGS:
            if self.ops[e]:
                lasts.append(self.ops[e][-1])
        last_dma = {}
        for e in ENGS:
            for o in self.ops[e]:
                if o.dma_key is not None:
                    last_dma[o.dma_key] = o
        deps = lasts + list(last_dma.values())
        bar = []
        for e in ENGS:
            bar.append(self.op(e, None, extra_deps=deps))
        for b in self.all_bufs:
            b.writer = None
            b.readers = []
        return bar

    def finalize_counts(self):
        for e in ENGS:
            c = 0
            for o in self.ops[e]:
                if o.dma_key is None and o.sig and o.fn is not None:
                    c += 1
                o.count = c


def build_nc(stop_after=None, debug=False, skip=(), b1_limit=None, dump=False):
    nc = bass.Bass("TRN2", target_bir_lowering=False)
    okind = "ExternalOutput" if debug else "Internal"

    def din(name, shape, dt=F32):
        return nc.dram_tensor(name, list(shape), dt, kind="ExternalInput").ap()

    xT_in = din("xT", [D, S])
    w_in = din("w_in", [DEPTH, 128, 8, WIN_COLS])
    w_out = din("w_out", [DEPTH, 128, 8, D])
    w_gate = din("w_gate", [DEPTH, 128, 8, DFF])
    w_up = din("w_up", [DEPTH, 128, 8, DFF])
    w_down = din("w_down", [DEPTH, 128, NF, D])
    g_mix = din("g_mix", [DEPTH, 128, 8])
    g_ffn = din("g_ffn", [DEPTH, 128, 8])
    g_fin = din("g_fin", [128, 8])
    g_finb = din("g_finb", [128, D])
    g_out = din("g_out", [DEPTH, 128, D])
    sinkb = din("sinkb", [DEPTH, 128, 8])
    qscale_d = din("qscale", [128, NQK])
    ident_d = din("ident", [128, 128], BF16)
    identf_d = din("identf", [128, 128], F32)
    ones_d = din("ones", [128, 128], BF16)
    ddt_d = din("ddt", [9, 128, 512], BF16)
    dw_d = din("dw", [128, 384], BF16)

    out_d = nc.dram_tensor("out", [S, D], F32, kind="ExternalOutput").ap()
    XS = nc.dram_tensor("xs", [D, S], F32, kind=okind).ap()
    QKT = nc.dram_tensor("qkt", [NQK * 128, S], BF16, kind=okind).ap()
    VS = nc.dram_tensor("vs", [S, VW], BF16, kind=okind).ap()
    OP = nc.dram_tensor("op", [4, S, 520], F32, kind=okind).ap()
    if debug:
        DBG_YN = nc.dram_tensor("dbg_yn", [S, D], BF16, kind=okind).ap()
        DBG_Y = nc.dram_tensor("dbg_y", [S, D], F32, kind=okind).ap()

    sch = Sched()
    stack = ExitStack()
    NB16 = 106000
    SB = stack.enter_context(nc.sbuf_tensor("sb", [128, NB16], BF16))
    SBF = SB.bitcast(F32)
    PS = [stack.enter_context(nc.psum_tensor(f"ps{i}", [128, 512], F32)) for i in range(8)]
    PSB = [p.bitcast(BF16) for p in PS]

    class Arena:
        def __init__(self, base=0):
            self.off = base

        def alloc(self, dt, *free):
            n = int(np.prod(free))
            sz = 2 if dt == BF16 else 4
            self.off = (self.off + 63) // 64 * 64
            off = self.off
            self.off += n * sz
            assert self.off <= NB16 * 2, f"SBUF overflow {self.off}"
            if dt == BF16:
                ap = SB[:, off // 2: off // 2 + n]
            else:
                ap = SBF[:, off // 4: off // 4 + n]
            if len(free) == 2:
                ap = ap.rearrange("p (a b) -> p a b", a=free[0], b=free[1])
            elif len(free) == 3:
                ap = ap.rearrange("p (a b c) -> p a b c", a=free[0], b=free[1], c=free[2])
            return ap

    def sub_ap(ap, part0, nparts, col_off, dims):
        pstride = ap.ap[0][0]
        return bass.AP(ap.tensor, ap.offset + part0 * pstride + col_off, [[pstride, nparts]] + dims)

    ar0 = Arena(0)
    ident = ar0.alloc(BF16, 128)
    identf = ar0.alloc(F32, 128)
    ones = ar0.alloc(BF16, 128)
    dwt = ar0.alloc(BF16, 384)
    qscale = ar0.alloc(F32, NQK)
    gfin = ar0.alloc(F32, 8)
    epsc = ar0.alloc(F32, 1)
    b_const = sch.buf("const")
    CONST_END = ar0.off

    def dma(eng, out, in_, key, reads=(), writes=()):
        return sch.op(eng, lambda e: e.dma_start(out=out, in_=in_), reads=reads, writes=writes, dma_key=key)

    dma("sp", ident, ident_d, "const", writes=[b_const])
    dma("sp", identf, identf_d, "const", writes=[b_const])
    dma("sp", ones, ones_d, "const", writes=[b_const])
    dma("sp", dwt, dw_d, "const", writes=[b_const])
    dma("sp", qscale, qscale_d, "const", writes=[b_const])
    dma("sp", gfin, g_fin, "const", writes=[b_const])
    sch.op("dve", lambda e: e.memset(epsc, EPS), writes=[b_const])
    sch.barrier()

    def xsrc(l):
        return xT_in if l == 0 else XS

    def load_x_tile(dst, src, it):
        return src.rearrange("(c p) t -> p c t", p=128)[:, :, it * 512:(it + 1) * 512]

    def emit_rstd(ps_buf, ps_ap, rstd_buf, rstd_ap, tmp_ap, n, tmp_buf):
        sch.op("act", lambda e: e.activation(out=tmp_ap, in_=ps_ap, func=AF.Sqrt, scale=1.0 / n, bias=epsc),
               reads=[ps_buf, b_const], writes=[tmp_buf])
        return sch.op("dve", lambda e: e.reciprocal(out=rstd_ap, in_=tmp_ap), reads=[tmp_buf], writes=[rstd_buf])

    def phase_A(l):
        ar = Arena(CONST_END)
        wA = ar.alloc(BF16, 8, WIN_COLS)
        gm = ar.alloc(F32, 8)
        xt = [ar.alloc(F32, 8, 512) for _ in range(2)]
        sq = [ar.alloc(BF16, 8, 512) for _ in range(2)]
        hT = [ar.alloc(BF16, 8, 512) for _ in range(2)]
        rstd = [ar.alloc(F32, 512) for _ in range(2)]
        rtmp = [ar.alloc(F32, 512) for _ in range(2)]
        stg = [ar.alloc(BF16, NQK, 512) for _ in range(2)]
        vst = [ar.alloc(BF16, 4, 10, 65) for _ in range(2)]
        b_w = sch.buf("wA")
        b_gm = sch.buf("gm")
        b_xt = [sch.buf("xt0"), sch.buf("xt1")]
        b_sq = [sch.buf("sq0"), sch.buf("sq1")]
        b_hT = [sch.buf("hT0"), sch.buf("hT1")]
        b_rstd = [sch.buf("rstd0"), sch.buf("rstd1")]
        b_rtmp = [sch.buf("rtmp0"), sch.buf("rtmp1")]
        b_stg = [sch.buf("stg0"), sch.buf("stg1")]
        b_vst = [sch.buf("vst0"), sch.buf("vst1")]
        b_ps = [sch.buf(f"psA{i}") for i in range(8)]

        src = xsrc(l)
        NT = S // 512
        dma("sp", xt[0], load_x_tile(None, src, 0), "xt0", writes=[b_xt[0]])
        dma("sp", gm, g_mix[l], "gm", writes=[b_gm])
        WBLK = [(0, 512), (512, 1024), (1024, 1792), (1792, WIN_COLS)]
        b_wb = [sch.buf(f"wA{k}") for k in range(4)]
        for k, (c0_, c1_) in enumerate(WBLK):
            for c in range(8):
                dma("pool", wA[:, c, c0_:c1_], w_in[l, :, c, c0_:c1_], f"wA{k}", writes=[b_wb[k]])
        for s in range(2):
            sch.op("dve", lambda e, s=s: e.memset(vst[s][:, :, :, 64:65], 1.0), writes=[b_vst[s]])

        def norm(it):
            s = it % 2
            sch.op("act", lambda e, s=s: e.activation(out=sq[s], in_=xt[s], func=AF.Square),
                   reads=[b_xt[s]], writes=[b_sq[s]])

            def f_ssq(e, s=s):
                ins = None
                for c in range(8):
                    ins = e.matmul(PS[7][:, :], lhsT=ones, rhs=sq[s][:, c, :], start=(c == 0), stop=(c == 7))
                return ins
            sch.op("pe", f_ssq, reads=[b_sq[s], b_const], writes=[b_ps[7]])
            emit_rstd(b_ps[7], PS[7][:, :], b_rstd[s], rstd[s], rtmp[s], D, b_rtmp[s])

            def f_h(e, s=s):
                ins = None
                for c in range(8):
                    ins = e.scalar_tensor_tensor(out=hT[s][:, c, :], in0=xt[s][:, c, :], scalar=gm[:, c:c + 1],
                                                 in1=rstd[s], op0=ALU.mult, op1=ALU.mult)
                return ins
            sch.op("dve", f_h, reads=[b_xt[s], b_rstd[s], b_gm], writes=[b_hT[s]])

        def qk_chunk(it, j):
            s = it % 2
            pb = j % 4

            def f_mm(e, j=j, pb=pb, s=s):
                ins = None
                for c in range(8):
                    ins = e.matmul(PS[pb][:, :], lhsT=wA[:, c, j * 128:(j + 1) * 128], rhs=hT[s][:, c, :],
                                   start=(c == 0), stop=(c == 7))
                return ins
            sch.op("pe", f_mm, reads=[b_hT[s], b_wb[0 if j < 4 else (1 if j < 8 else 2)]], writes=[b_ps[pb]])
            is_q = j < 4 or 8 <= j < 12
            if is_q:
                sch.op("dve", lambda e, j=j, pb=pb, s=s: e.tensor_scalar(
                    out=stg[s][:, j, :], in0=PS[pb][:, :], scalar1=qscale[:, j:j + 1], scalar2=None,
                    op0=ALU.mult), reads=[b_ps[pb], b_const], writes=[b_stg[s]])
            else:
                sch.op("act", lambda e, j=j, pb=pb, s=s: e.activation(
                    out=stg[s][:, j, :], in_=PS[pb][:, :], func=AF.Copy),
                    reads=[b_ps[pb]], writes=[b_stg[s]])

        norm(0)
        for it in range(NT):
            s = it % 2
            if it + 1 < NT:
                dma("sp", xt[1 - s], load_x_tile(None, src, it + 1), f"xt{1 - s}", writes=[b_xt[1 - s]])
            for j in range(7):
                qk_chunk(it, j)
            if it + 1 < NT:
                norm(it + 1)
            for j in range(7, NQK):
                qk_chunk(it, j)
            dma("pool", QKT.rearrange("(j p) t -> p j t", p=128)[:, :, it * 512:(it + 1) * 512], stg[s],
                f"stgst{s}", reads=[b_stg[s]])
            for sub in range(4):
                pa = 4 + (sub % 2)

                def f_v(e, sub=sub, pa=pa, s=s):
                    ins = None
                    for c in range(8):
                        ins = e.matmul(PS[pa][:, :], lhsT=hT[s][:, c, sub * 128:(sub + 1) * 128],
                                       rhs=wA[:, c, 1792:2304], start=(c == 0), stop=(c == 7))
                    return ins
                sch.op("pe", f_v, reads=[b_hT[s], b_wb[3]], writes=[b_ps[pa]])
                sch.op("act", lambda e, sub=sub, pa=pa, s=s: e.activation(
                    out=vst[s][:, sub, 0:8, 0:64], in_=PS[pa][:, :].rearrange("p (h d) -> p h d", h=8),
                    func=AF.Copy), reads=[b_ps[pa]], writes=[b_vst[s]])

                def f_v2(e, sub=sub, s=s):
                    ins = None
                    for c in range(8):
                        ins = e.matmul(PS[6][:, 0:128], lhsT=hT[s][:, c, sub * 128:(sub + 1) * 128],
                                       rhs=wA[:, c, 2304:2432], start=(c == 0), stop=(c == 7))
                    return ins
                sch.op("pe", f_v2, reads=[b_hT[s], b_wb[3]], writes=[b_ps[6]])
                sch.op("dve", lambda e, sub=sub, s=s: e.tensor_copy(
                    out=vst[s][:, sub, 8:10, 0:64], in_=PS[6][:, 0:128].rearrange("p (h d) -> p h d", h=2)),
                    reads=[b_ps[6]], writes=[b_vst[s]])
            dma("pool", VS[it * 512:(it + 1) * 512, :].rearrange("(u p) w -> p u w", p=128),
                vst[s].rearrange("p u h d -> p u (h d)"), f"vstst{s}", reads=[b_vst[s]])
        sch.barrier()

    WG_OFF, WU_OFF, WD_OFF = 76800, 121856, 166912
    assert WD_OFF + NF * D * 2 <= NB16 * 2

    def at(off, dt, *free):
        a_ = Arena(off)
        return a_.alloc(dt, *free)

    wg = at(WG_OFF, BF16, 8, DFF)
    wu = at(WU_OFF, BF16, 8, DFF)
    wd = at(WD_OFF, BF16, NF, D)
    b_wC = sch.buf("wC")
    LOOK = 2

    def prefetch_list(l):
        lst = [(wu[:, c, :], w_up[l, :, c, :]) for c in range(8)]
        lst += [(wd[:, f, :], w_down[l, :, f, :]) for f in range(NF)]
        return lst

    def prefetch_C(l, which):
        if which == "g":
            for c in range(2, 8):
                dma("pool", wg[:, c, :], w_gate[l, :, c, :], "wC", writes=[b_wC])
        if which == "g01":
            for c in range(2):
                dma("pool", wg[:, c, :], w_gate[l, :, c, :], "wC", writes=[b_wC])

    def phase_Bw(l):
        ar = Arena(CONST_END)
        QA = ar.alloc(BF16, 4, S)
        KA = ar.alloc(BF16, 4, S)
        QB = ar.alloc(BF16, 4, S)
        KB = ar.alloc(BF16, 2, S)
        VB = ar.alloc(BF16, 32, 130)
        pTw = [ar.alloc(BF16, 384) for _ in range(4)]
        ostw = [ar.alloc(F32, 520) for _ in range(2)]
        b_qk = sch.buf("qkB")
        b_qkA = sch.buf("qkA")
        b_VB = sch.buf("VB")
        b_pTw = [sch.buf(f"pTw{i}") for i in range(2)]
        b_Sw = [sch.buf(f"Sw{i}") for i in range(2)]
        b_oB = [sch.buf(f"oB{i}") for i in range(4)]
        b_ostw = [sch.buf(f"ostw{i}") for i in range(2)]
        qkv = QKT.rearrange("(j p) t -> p j t", p=128)
        for j in range(4):
            dma("sp", QB[:, j, :], qkv[:, 8 + j, :], "qkB", writes=[b_qk])
        for j in range(2):
            dma("sp", KB[:, j, :], qkv[:, 12 + j, :], "qkB", writes=[b_qk])
        for q4 in range(4):
            src = bass.AP(VS.tensor, VS.offset + q4 * 1024 * VW + 520, [[VW, 128], [VW * 128, 8], [1, 130]])
            dma("sp", VB[:, q4 * 8:(q4 + 1) * 8, :], src, "VB", writes=[b_VB])
        for j in range(4):
            dma("sp", QA[:, j, :], qkv[:, j, :], "qkA", writes=[b_qkA])
            dma("sp", KA[:, j, :], qkv[:, 4 + j, :], "qkA", writes=[b_qkA])

        NTT = S // 128
        items = [(i, c) for i in range(NTT) for c in range(4)]
        N = len(items)
        LK = 1

        def jbs_of(i):
            return [jb for jb in range(3) if 0 <= i - 1 + jb < NTT]

        for k in range(N + LK):
            if k < N:
                i, c = items[k]
                g = c // 2
                slot = k % 2
                jbs = jbs_of(i)
                c0, c1 = jbs[0] * 128, jbs[-1] * 128 + 128

                def f_s(e, i=i, c=c, g=g, slot=slot, jbs=jbs, c0=c0, c1=c1):
                    ins = None
                    for hh in range(2):
                        e.matmul(PS[2 * slot + hh][:, c0:c1], lhsT=ident, rhs=dwt[:, c0:c1], start=True, stop=False)
                    for jb in jbs:
                        j = i - 1 + jb
                        for hh in range(2):
                            ksel = 0 if g == hh else 1
                            kk = sub_ap(KB, 64 * hh, 64, ksel * S + j * 128, [[1, 128]])
                            qv = sub_ap(QB, 64 * hh, 64, c * S + i * 128, [[1, 128]])
                            ins = e.matmul(PS[2 * slot + hh][:, jb * 128:(jb + 1) * 128], lhsT=kk, rhs=qv,
                                           start=False, stop=(jb == jbs[-1]))
                    return ins
                sch.op("pe", f_s, reads=[b_qk, b_const], writes=[b_Sw[slot]])

                def f_e(e, c=c, slot=slot, c0=c0, c1=c1):
                    ins = None
                    for hh in range(2):
                        ins = e.activation(out=pTw[2 * slot + hh][:, c0:c1], in_=PS[2 * slot + hh][:, c0:c1],
                                           func=AF.Exp, scale=float(SL_WIN[2 * c + hh]))
                    return ins
                sch.op("act", f_e, reads=[b_Sw[slot]], writes=[b_pTw[slot]])
            kk_ = k - LK
            if kk_ >= 0:
                i, c = items[kk_]
                g = c // 2
                slot = kk_ % 2
                jbs = jbs_of(i)
                ob = (i % 2) * 2 + c // 2

                def f_pv(e, slot=slot, jbs=jbs, i=i, g=g, c=c, ob=ob):
                    ins = None
                    for hh in range(2):
                        h = 2 * c + hh
                        ops_ = PS[4 + ob][:, (h % 4) * 65:(h % 4) * 65 + 65]
                        for n_, jb in enumerate(jbs):
                            j = i - 1 + jb
                            ins = e.matmul(ops_, lhsT=pTw[2 * slot + hh][:, jb * 128:(jb + 1) * 128],
                                           rhs=VB[:, j, g * 65:(g + 1) * 65], start=(n_ == 0),
                                           stop=(n_ == len(jbs) - 1))
                    return ins
                sch.op("pe", f_pv, reads=[b_pTw[slot], b_VB], writes=[b_oB[ob]])
                if c == 3:
                    os_ = i % 2
                    ob0 = (i % 2) * 2

                    def f_ev(e, os_=os_, ob0=ob0):
                        e.tensor_copy(out=ostw[os_][:, 0:260], in_=PS[4 + ob0][:, 0:260])
                        return e.tensor_copy(out=ostw[os_][:, 260:520], in_=PS[5 + ob0][:, 0:260])
                    sch.op("dve", f_ev, reads=[b_oB[ob0], b_oB[ob0 + 1]], writes=[b_ostw[os_]])
                    dma("pool", OP[3, i * 128:(i + 1) * 128, :], ostw[os_], f"ostw{os_}", reads=[b_ostw[os_]])
        sch.barrier()
        return QA, KA

    def phase_Bd(l, QA, KA):
        pf = prefetch_list(l)
        ar = Arena(CONST_END + 2 * 4 * S * 2)
        vown = [ar.alloc(BF16, 8, 65) for _ in range(3)]
        vcmp = [ar.alloc(BF16, 8, 65) for _ in range(3)]
        pT = [ar.alloc(BF16, 512) for _ in range(4)]
        KC = [ar.alloc(BF16, 4, 128) for _ in range(3)]
        ost = [ar.alloc(F32, 520) for _ in range(2)]
        ddt = ar.alloc(BF16, 9, 512)
        LOOK = 1
        assert ar.off <= WU_OFF
        b_ddt = sch.buf("ddt")
        dma("sp", ddt, ddt_d.rearrange("a p c -> p a c"), "ddt", writes=[b_ddt])
        b_qk = sch.buf("qkA2")
        b_kc = [sch.buf(f"kc{i}") for i in range(3)]
        b_vown = [sch.buf(f"vown{i}") for i in range(3)]
        b_vcmp = [sch.buf(f"vcmp{i}") for i in range(3)]
        b_pT = [sch.buf(f"pT{i}") for i in range(4)]
        b_ost = [sch.buf(f"ost{i}") for i in range(2)]
        b_S = [sch.buf(f"S{i}") for i in range(4)]
        b_o = [sch.buf(f"o{i}") for i in range(4)]

        tiles = []
        for p, dil in enumerate(DILS):
            L = S // dil
            for r in range(dil):
                for m in range(L // 128):
                    tiles.append((p, dil, L, r, m))
        if b1_limit is not None:
            tiles = [tiles[k] for k in b1_limit]

        def geom(ti):
            p, dil, L, r, m = tiles[ti]
            nm = L // 128
            la = 128 * m - 64 if m > 0 else 0
            lb = 128 * m + 128 if m < nm - 1 else 128 * m
            var = 1 if m == 0 else (2 if m == nm - 1 else 0)
            t0 = r + dil * 128 * m
            return p, dil, r, m, la, lb, var, t0

        def v_loads(ti):
            slot = ti % 3
            p, dil, r, m, la, lb, var, t0 = geom(ti)
            src = bass.AP(VS.tensor, VS.offset + t0 * VW, [[dil * VW, 128], [1, 520]])
            dma("sp", vown[slot].rearrange("p h d -> p (h d)"), src, f"vown{slot}", writes=[b_vown[slot]])
            for half, l0 in ((0, la), (1, lb)):
                src = bass.AP(VS.tensor, VS.offset + (r + dil * l0) * VW, [[dil * VW, 64], [1, 520]])
                dma("sp", vcmp[slot][64 * half:64 * half + 64].rearrange("p h d -> p (h d)"), src,
                    f"vcmp{slot}", writes=[b_vcmp[slot]])

            def f_kc(e, slot=slot, r=r, dil=dil, la=la, lb=lb):
                ins = None
                for half, l0 in ((0, la), (1, lb)):
                    ins = e.tensor_copy(out=KC[slot][:, :, 64 * half:64 * half + 64],
                                        in_=sub_ap(KA, 0, 128, r + dil * l0, [[S, 4], [dil, 64]]))
                return ins
            sch.op("dve", f_kc, reads=[b_qk], writes=[b_kc[slot]])

        for ti in range(min(3, len(tiles))):
            v_loads(ti)
        items = [(ti, c) for ti in range(len(tiles)) for c in range(4)]
        N = len(items)
        LOOK = 1
        for k in range(N + LOOK):
            if k < N:
                ti, c = items[k]
                p, dil, r, m, la, lb, var, t0 = geom(ti)
                slot = ti % 3
                sslot = k % 2

                def f_s(e, sslot=sslot, slot=slot, c=c, p=p, var=var, t0=t0, dil=dil):
                    for hh in range(2):
                        e.matmul(PS[2 * sslot + hh][:, 0:256], lhsT=ident, rhs=ddt[:, p * 3 + var, 0:256],
                                 start=True, stop=False)
                    ins = None
                    for part in range(2):
                        for hh in range(2):
                            qv = sub_ap(QA, 64 * hh, 64, c * S + t0, [[dil, 128]])
                            if part == 0:
                                kx = sub_ap(KA, 64 * hh, 64, c * S + t0, [[dil, 128]])
                            else:
                                kx = KC[slot][64 * hh:64 * hh + 64, c, :]
                            ins = e.matmul(PS[2 * sslot + hh][:, part * 128:part * 128 + 128], lhsT=kx, rhs=qv,
                                           start=False, stop=(part == 1))
                    return ins
                sch.op("pe", f_s, reads=[b_qk, b_const, b_kc[slot], b_ddt], writes=[b_S[sslot]])

                def f_e(e, sslot=sslot, c=c):
                    ins = None
                    for hh in range(2):
                        ins = e.activation(out=pT[sslot][:, hh * 256:(hh + 1) * 256],
                                           in_=PS[2 * sslot + hh][:, 0:256], func=AF.Exp,
                                           scale=float(SL_DIL[2 * c + hh]))
                    return ins
                sch.op("act", f_e, reads=[b_S[sslot]], writes=[b_pT[sslot]])
            kk_ = k - LOOK
            if kk_ >= 0:
                ti, c = items[kk_]
                p, dil, r, m, la, lb, var, t0 = geom(ti)
                slot = ti % 3
                pslot = kk_ % 2
                ob = (ti % 2) * 2
                obank = 4 + ob + c // 2

                def f_pv(e, pslot=pslot, slot=slot, c=c, obank=obank):
                    ins = None
                    for hh in range(2):
                        h = 2 * c + hh
                        ops_ = PS[obank][:, (h % 4) * 65:(h % 4) * 65 + 65]
                        e.matmul(ops_, lhsT=pT[pslot][:, hh * 256:hh * 256 + 128], rhs=vown[slot][:, h, :],
                                 start=True, stop=False)
                        ins = e.matmul(ops_, lhsT=pT[pslot][:, hh * 256 + 128:hh * 256 + 256],
                                       rhs=vcmp[slot][:, h, :], start=False, stop=True)
                    return ins
                sch.op("pe", f_pv, reads=[b_pT[pslot], b_vown[slot], b_vcmp[slot]], writes=[b_o[ob + c // 2]])
                if c == 3:
                    os_ = ti % 2

                    def f_ev(e, os_=os_, ob=ob):
                        e.tensor_copy(out=ost[os_][:, 0:260], in_=PS[4 + ob][:, 0:260])
                        return e.tensor_copy(out=ost[os_][:, 260:520], in_=PS[5 + ob][:, 0:260])
                    sch.op("dve", f_ev, reads=[b_o[ob], b_o[ob + 1]], writes=[b_ost[os_]])
                    dst = bass.AP(OP.tensor, OP.offset + p * S * 520 + t0 * 520, [[dil * 520, 128], [1, 520]])
                    dma("sp", dst, ost[os_], f"ostst{os_}", reads=[b_ost[os_]])
                    if ti + 3 < len(tiles):
                        v_loads(ti + 3)
                    if pf:
                        o_, i_ = pf.pop(0)
                        dma("pool", o_, i_, "wC", writes=[b_wC])
        while pf:
            o_, i_ = pf.pop(0)
            dma("pool", o_, i_, "wC", writes=[b_wC])
        sch.barrier()

    def phase_Be(l):
        ar = Arena(CONST_END)
        wo = ar.alloc(BF16, 8, D)
        gob = ar.alloc(F32, D)
        esink = ar.alloc(F32, 8)
        NOS = 3
        osl = [[ar.alloc(F32, 520) for _ in range(4)] for _ in range(NOS)]
        ya = [ar.alloc(F32, 512) for _ in range(2)]
        yb = [ar.alloc(F32, 512) for _ in range(2)]
        sm = [ar.alloc(F32, 32) for _ in range(2)]
        yn = [ar.alloc(BF16, D) for _ in range(2)]
        ynT = ar.alloc(BF16, 8, 512)
        xtb = ar.alloc(F32, 8, 512)
        assert ar.off <= WG_OFF + 2 * DFF * 2, ar.off
        b_wo = sch.buf("wo")
        b_small = sch.buf("smallB")
        b_osl = [sch.buf(f"osl{i}") for i in range(NOS)]
        b_ep = [sch.buf("ep0"), sch.buf("ep1")]
        b_yn = [sch.buf("yn0"), sch.buf("yn1")]
        b_ynT = sch.buf("ynT")
        b_xtb = sch.buf("xtB")
        b_tp = [sch.buf("tp0"), sch.buf("tp1")]
        b_op = [sch.buf("opj0"), sch.buf("opj1")]
        for c in range(8):
            dma("pool", wo[:, c, :], w_out[l, :, c, :], "wo", writes=[b_wo])
        prefetch_C(l, "g")
        dma("sp", gob, g_out[l], "smallB", writes=[b_small])
        dma("sp", esink, sinkb[l], "smallB", writes=[b_small])
        sch.op("act", lambda e: e.activation(out=esink, in_=esink, func=AF.Exp), reads=[b_small], writes=[b_small])

        def o_loads(i):
            s = i % NOS
            for p in range(4):
                dma("sp", osl[s][p], OP[p, i * 128:(i + 1) * 128, :], f"osl{s}", writes=[b_osl[s]])

        def bc(ap8, n, stride=1):
            return bass.AP(ap8.tensor, ap8.offset, [list(ap8.ap[0]), [stride, n], [0, 64]])

        src = xsrc(l)
        NTT = S // 128

        def stage_X(i):
            s = i % 2
            so = i % NOS
            o0, o1, o2, o3 = osl[so]
            o0v = o0.rearrange("p (h d) -> p h d", h=8)
            o3v = o3.rearrange("p (h d) -> p h d", h=8)
            sm_, ya_, yb_, yn_ = sm[s], ya[s], yb[s], yn[s]

            def f_1(e):
                e.memset(sm_[:, 24:26], 0.0)
                e.tensor_tensor(out=sm_[:, 8:16], in0=o3v[:, :, 64], in1=esink, op=ALU.add)
                return e.tensor_tensor(out=o0, in0=o0, in1=o1, op=ALU.add)
            sch.op("dve", f_1, reads=[b_osl[so], b_small], writes=[b_osl[so], b_ep[s]])

            def f_2(e):
                e.reciprocal(out=sm_[:, 16:24], in_=sm_[:, 8:16])
                return e.tensor_tensor(out=o0, in0=o0, in1=o2, op=ALU.add)
            sch.op("dve", f_2, reads=[b_osl[so], b_ep[s]], writes=[b_osl[so], b_ep[s]])

            def f_3(e):
                e.reciprocal(out=sm_[:, 0:8], in_=o0v[:, :, 64])
                return e.tensor_tensor(out=yb_.rearrange("p (h d) -> p h d", h=8), in0=o3v[:, :, 0:64],
                                       in1=bc(sm_[:, 16:24], 8), op=ALU.mult)
            sch.op("dve", f_3, reads=[b_osl[so], b_ep[s]], writes=[b_ep[s]])
            sch.op("dve", lambda e: e.tensor_tensor(
                out=ya_.rearrange("p (h d) -> p h d", h=8), in0=o0v[:, :, 0:64], in1=bc(sm_[:, 0:8], 8),
                op=ALU.mult), reads=[b_osl[so], b_ep[s]], writes=[b_ep[s]])
            if i + NOS < NTT:
                o_loads(i + NOS)

            def f_ss(e):
                e.activation(out=yn_[:, 0:512], in_=ya_, func=AF.Square, accum_out=sm_[:, 24:25])
                return e.activation(out=yn_[:, 512:1024], in_=yb_, func=AF.Square, accum_out=sm_[:, 25:26])
            sch.op("act", f_ss, reads=[b_ep[s]], writes=[b_ep[s], b_yn[s]])
            sch.op("act", lambda e: e.activation(out=sm_[:, 26:28], in_=sm_[:, 24:26], func=AF.Sqrt, scale=1.0 / 512,
                                                 bias=epsc), reads=[b_ep[s], b_const], writes=[b_ep[s]])

        def stage_Y(i):
            s = i % 2
            big = i // 4
            sub = i % 4
            sm_, ya_, yb_, yn_ = sm[s], ya[s], yb[s], yn[s]
            if sub == 0:
                dma("sp", xtb, load_x_tile(None, src, big), "xtB0", writes=[b_xtb])

            sch.op("dve", lambda e: e.reciprocal(out=sm_[:, 28:30], in_=sm_[:, 26:28]),
                   reads=[b_ep[s]], writes=[b_ep[s]])

            def f_4(e):
                e.scalar_tensor_tensor(out=yn_[:, 0:512], in0=ya_, scalar=sm_[:, 28:29], in1=gob[:, 0:512],
                                       op0=ALU.mult, op1=ALU.mult)
                return e.scalar_tensor_tensor(out=yn_[:, 512:1024], in0=yb_, scalar=sm_[:, 29:30],
                                              in1=gob[:, 512:1024], op0=ALU.mult, op1=ALU.mult)
            sch.op("dve", f_4, reads=[b_ep[s], b_small], writes=[b_yn[s]])
            if debug and dump:
                dma("pool", DBG_YN[i * 128:(i + 1) * 128, :], yn_, f"dbg0{s}", reads=[b_yn[s]])
                dma("pool", DBG_Y[i * 128:(i + 1) * 128, 0:512], ya_, f"dbg1{s}", reads=[b_ep[s]])
                dma("pool", DBG_Y[i * 128:(i + 1) * 128, 512:1024], yb_, f"dbg2{s}", reads=[b_ep[s]])
            tb = i % 2

            def f_tp(e, tb=tb):
                ins = None
                for c in range(8):
                    ins = e.transpose(out=PSB[6 + tb][:, c * 128:(c + 1) * 128], in_=yn_[:, c * 128:(c + 1) * 128],
                                      identity=ident)
                return ins
            sch.op("pe", f_tp, reads=[b_yn[s], b_const], writes=[b_tp[tb]])
            sch.op("act", lambda e, sub=sub, tb=tb: e.activation(
                out=ynT[:, :, sub * 128:(sub + 1) * 128],
                in_=PSB[6 + tb][:, 0:1024].rearrange("p (c t) -> p c t", c=8), func=AF.Copy),
                reads=[b_tp[tb]], writes=[b_ynT])
            if sub == 3:
                for d in range(8):
                    pb = d % 2

                    def f_o(e, d=d, pb=pb):
                        ins = None
                        for c in range(8):
                            ins = e.matmul(PS[4 + pb][:, :], lhsT=wo[:, c, d * 128:(d + 1) * 128], rhs=ynT[:, c, :],
                                           start=(c == 0), stop=(c == 7))
                        return ins
                    sch.op("pe", f_o, reads=[b_ynT, b_wo], writes=[b_op[pb]])
                    sch.op("dve", lambda e, d=d, pb=pb: e.tensor_tensor(
                        out=xtb[:, d, :], in0=PS[4 + pb][:, :], in1=xtb[:, d, :], op=ALU.add),
                        reads=[b_op[pb], b_xtb], writes=[b_xtb])
                dma("pool", XS.rearrange("(c p) t -> p c t", p=128)[:, :, big * 512:(big + 1) * 512], xtb,
                    "xtBst0", reads=[b_xtb])

        for i0 in range(NOS):
            o_loads(i0)
        stage_X(0)
        for i in range(NTT):
            if i + 1 < NTT:
                stage_X(i + 1)
            stage_Y(i)
        sch.barrier()

    def phase_B(l):
        QA, KA = phase_Bw(l)
        if stop_after == ("Bw", l):
            return
        phase_Bd(l, QA, KA)
        if stop_after == ("B1", l):
            return
        phase_Be(l)

    def phase_C(l):
        last = (l == DEPTH - 1)
        ar = Arena(CONST_END)
        gm = ar.alloc(F32, 8)
        xt = [ar.alloc(F32, 8, 512) for _ in range(2)]
        hT = ar.alloc(BF16, 8, 512)
        ar.off = (ar.off + 63) // 64 * 64
        aoff = ar.off
        actT = ar.alloc(BF16, NF, 512)
        sqf = actT[:, 0:8, :]
        ostf = [SBF[:, aoff // 4 + 2048 + k * 1024: aoff // 4 + 2048 + (k + 1) * 1024] for k in range(2)]
        rstd = ar.alloc(F32, 512)
        sg = [ar.alloc(F32, 512) for _ in range(2)]
        if last:
            gfb = ar.alloc(F32, D)
            smf = [ar.alloc(F32, 8) for _ in range(2)]
        assert ar.off <= WG_OFF, ar.off
        b_w = b_wC
        b_gm = sch.buf("gmC")
        b_xt = [sch.buf("xtC0"), sch.buf("xtC1")]
        b_hT = sch.buf("hTC")
        b_act = sch.buf("actT")
        b_rstd = sch.buf("rstdC")
        b_gfb = sch.buf("gfb")
        b_ostf = [sch.buf("ostf0"), sch.buf("ostf1")]
        b_smf = [sch.buf("smf0"), sch.buf("smf1")]
        if last:
            dma("sp", gfb, g_finb, "gfb", writes=[b_gfb])
        b_sg = [sch.buf("sg0"), sch.buf("sg1")]
        b_ps = [sch.buf(f"psC{i}") for i in range(8)]
        prefetch_C(l, "g01")
        dma("sp", gm, g_ffn[l], "gmC", writes=[b_gm])
        NT = S // 512
        dma("sp", xt[0], load_x_tile(None, XS, 0), "xtC0", writes=[b_xt[0]])

        def norm_sq(it):
            s = it % 2
            sch.op("act", lambda e, s=s: e.activation(out=hT, in_=xt[s], func=AF.Square),
                   reads=[b_xt[s]], writes=[b_hT])

        def norm_rest(it):
            s = it % 2

            def f_ssq(e):
                ins = None
                for c in range(8):
                    ins = e.matmul(PS[7][:, :], lhsT=ones, rhs=hT[:, c, :], start=(c == 0), stop=(c == 7))
                return ins
            sch.op("pe", f_ssq, reads=[b_hT, b_const], writes=[b_ps[7]])
            emit_rstd(b_ps[7], PS[7][:, :], b_rstd, rstd, sg[0], D, tmp_buf=b_sg[0])

            def f_h(e, s=s):
                ins = None
                for c in range(8):
                    ins = e.scalar_tensor_tensor(out=hT[:, c, :], in0=xt[s][:, c, :], scalar=gm[:, c:c + 1],
                                                 in1=rstd, op0=ALU.mult, op1=ALU.mult)
                return ins
            sch.op("dve", f_h, reads=[b_xt[s], b_rstd, b_gm], writes=[b_hT])

        def down(it, d):
            s = it % 2
            pd = 4 + d % 2

            def f_d(e, d=d, pd=pd):
                ins = None
                for f in range(NF):
                    ins = e.matmul(PS[pd][:, :], lhsT=wd[:, f, d * 128:(d + 1) * 128], rhs=actT[:, f, :],
                                   start=(f == 0), stop=(f == NF - 1))
                return ins
            sch.op("pe", f_d, reads=[b_act, b_w], writes=[b_ps[pd]])
            sch.op("dve", lambda e, d=d, pd=pd, s=s: e.tensor_tensor(
                out=xt[s][:, d, :], in0=PS[pd][:, :], in1=xt[s][:, d, :], op=ALU.add),
                reads=[b_ps[pd], b_xt[s]], writes=[b_xt[s]])

        def final_sub(it, sub):
            s = it % 2
            osb = sub % 2
            ostg = ostf[osb]
            smx = smf[osb]
            bk = [4 + 2 * (sub % 2), 5 + 2 * (sub % 2)]

            def f_t(e, sub=sub, bk=bk, s=s):
                ins = None
                for c in range(8):
                    ins = e.transpose(out=PS[bk[c // 4]][:, (c % 4) * 128:(c % 4 + 1) * 128],
                                      in_=xt[s][:, c, sub * 128:(sub + 1) * 128], identity=identf)
                return ins
            sch.op("pe", f_t, reads=[b_xt[s], b_const], writes=[b_ps[bk[0]], b_ps[bk[1]]])
            sch.op("dve", lambda e, smx=smx: e.memset(smx[:, 0:2], 0.0), writes=[b_smf[osb]])

            def f_q(e, bk=bk, ostg=ostg, smx=smx):
                e.activation(out=ostg[:, 0:512], in_=PS[bk[0]][:, :], func=AF.Square, accum_out=smx[:, 0:1])
                return e.activation(out=ostg[:, 512:1024], in_=PS[bk[1]][:, :], func=AF.Square,
                                    accum_out=smx[:, 1:2])
            sch.op("act", f_q, reads=[b_ps[bk[0]], b_ps[bk[1]]], writes=[b_smf[osb], b_ostf[osb]])
            sch.op("dve", lambda e, smx=smx: e.tensor_tensor(out=smx[:, 2:3], in0=smx[:, 0:1], in1=smx[:, 1:2],
                                                            op=ALU.add),
                   reads=[b_smf[osb]], writes=[b_smf[osb]])
            sch.op("act", lambda e, smx=smx: e.activation(out=smx[:, 3:4], in_=smx[:, 2:3], func=AF.Sqrt,
                                                          scale=1.0 / D, bias=epsc),
                   reads=[b_smf[osb], b_const], writes=[b_smf[osb]])
            sch.op("dve", lambda e, smx=smx: e.reciprocal(out=smx[:, 4:5], in_=smx[:, 3:4]),
                   reads=[b_smf[osb]], writes=[b_smf[osb]])

            def f_y(e, bk=bk, ostg=ostg, smx=smx):
                e.scalar_tensor_tensor(out=ostg[:, 0:512], in0=PS[bk[0]][:, :], scalar=smx[:, 4:5],
                                       in1=gfb[:, 0:512], op0=ALU.mult, op1=ALU.mult)
                return e.scalar_tensor_tensor(out=ostg[:, 512:1024], in0=PS[bk[1]][:, :], scalar=smx[:, 4:5],
                                              in1=gfb[:, 512:1024], op0=ALU.mult, op1=ALU.mult)
            sch.op("dve", f_y, reads=[b_ps[bk[0]], b_ps[bk[1]], b_smf[osb], b_gfb], writes=[b_ostf[osb]])
            o = dma("pool", out_d[it * 512 + sub * 128: it * 512 + (sub + 1) * 128, :], ostg,
                    f"outst{osb}", reads=[b_ostf[osb]])
            out_dmas.append(o)

        norm_sq(0)
        norm_rest(0)
        for it in range(NT):
            s = it % 2
            defer_load = last and it >= 1
            if it + 1 < NT and not defer_load:
                dma("sp", xt[1 - s], load_x_tile(None, XS, it + 1), f"xtC{1 - s}", writes=[b_xt[1 - s]])
            for f in range(NF):
                pg = (f % 2) * 2

                def f_g(e, f=f, pg=pg):
                    for c in range(8):
                        e.matmul(PS[pg][:, :], lhsT=wg[:, c, f * 128:(f + 1) * 128], rhs=hT[:, c, :],
                                 start=(c == 0), stop=(c == 7))
                    ins = None
                    for c in range(8):
                        ins = e.matmul(PS[pg + 1][:, :], lhsT=wu[:, c, f * 128:(f + 1) * 128], rhs=hT[:, c, :],
                                       start=(c == 0), stop=(c == 7))
                    return ins
                sch.op("pe", f_g, reads=[b_hT, b_w], writes=[b_ps[pg], b_ps[pg + 1]])
                sch.op("act", lambda e, f=f, pg=pg: e.activation(out=sg[f % 2], in_=PS[pg][:, :], func=AF.Silu),
                       reads=[b_ps[pg]], writes=[b_sg[f % 2]])
                sch.op("dve", lambda e, f=f, pg=pg: e.tensor_tensor(
                    out=actT[:, f, :], in0=PS[pg + 1][:, :], in1=sg[f % 2], op=ALU.mult),
                    reads=[b_sg[f % 2], b_ps[pg + 1]],
                    writes=[b_act] + (b_ostf if (last and 8 <= f < 16) else []))
                if last and it >= 1 and f in (1, 3):
                    final_sub(it - 1, 2 if f == 1 else 3)
                    if f == 3 and it + 1 < NT:
                        dma("sp", xt[1 - s], load_x_tile(None, XS, it + 1), f"xtC{1 - s}", writes=[b_xt[1 - s]])
            if it + 1 < NT:
                norm_sq(it + 1)
            for d in range(4):
                down(it, d)
            if it + 1 < NT:
                norm_rest(it + 1)
            for d in range(4, 8):
                down(it, d)
            if not last:
                dma("pool", XS.rearrange("(c p) t -> p c t", p=128)[:, :, it * 512:(it + 1) * 512], xt[s],
                    f"xtCst{s}", reads=[b_xt[s]])
            else:
                for sub in ((0, 1) if it + 1 < NT else (0, 1, 2, 3)):
                    final_sub(it, sub)
        sch.barrier()

    out_dmas = []

    done = False
    for l in range(DEPTH):
        if done:
            break
        if "A" not in skip:
            phase_A(l)
        if stop_after == ("A", l):
            break
        phase_B(l)
        if stop_after in (("Bw", l), ("B1", l), ("B", l)):
            break
        phase_C(l)
        if stop_after == ("C", l):
            break

    sch.finalize_counts()
    eng_sems = {e: stack.enter_context(nc.semaphore(f"s_{e}")) for e in ENGS}
    dma_sems = {k: stack.enter_context(nc.semaphore(f"d_{k}")) for k in sch.dma_counts}
    handles = {"pe": "tensor", "act": "scalar", "dve": "vector", "pool": "gpsimd", "sp": "sync"}

    def replay(en, e):
        waited = {}

        def wait(sem, val):
            key = sem.name
            if waited.get(key, 0) < val:
                e.wait_ge(sem, val)
                waited[key] = val

        for o in sch.ops[en]:
            for d in o.deps:
                if d.dma_key is not None:
                    wait(dma_sems[d.dma_key], d.dma_count)
                else:
                    if d.fn is None:
                        continue
                    if d.eng == en:
                        if en == "pe" or not SAME_ENGINE_SYNC:
                            continue
                    wait(eng_sems[d.eng], d.count)
            if o.fn is None:
                continue
            ins = o.fn(e)
            if o.dma_key is not None:
                ins.then_inc(dma_sems[o.dma_key], 16)
            elif o.sig:
                ins.then_inc(eng_sems[en], 1)
        if en == "sp":
            for k, n in sch.dma_counts.items():
                wait(dma_sems[k], n)

    with nc.Block() as block:
        @block.tensor
        def _(e):
            replay("pe", e)

        @block.scalar
        def _(e):
            replay("act", e)

        @block.vector
        def _(e):
            replay("dve", e)

        @block.gpsimd
        def _(e):
            replay("pool", e)

        @block.sync
        def _(e):
            replay("sp", e)
    stack.close()
    return nc


def _chunked(w, nchunk):
    K, N = w.shape
    return np.ascontiguousarray(w.reshape(nchunk, 128, N).transpose(1, 0, 2))


def _vecT(g):
    return np.ascontiguousarray(g.reshape(-1, 128).T)


def _const_tables():
    bf = ml_dtypes.bfloat16
    ident = np.eye(128, dtype=np.float32)
    ones = np.ones((128, 128), np.float32)
    k = np.arange(128)[:, None].astype(np.float32)
    q = np.arange(128)[None, :].astype(np.float32)
    dd_own = np.zeros((3, 128, 128), np.float32)
    dd_comp = np.zeros((3, 3, 128, 128), np.float32)
    for p, dil in enumerate(DILS):
        dist = np.abs(k - q)
        dd_own[p] = np.where(dist <= 64, -dil * dist, -BIG)
        i = np.arange(128)[:, None].astype(np.float32)
        j = np.arange(128)[None, :].astype(np.float32)
        d_prev = 64 + j - i
        d_next = 64 + i - j
        full = np.where(i < 64, d_prev, d_next)
        base = np.where(full <= 64, -dil * full, -BIG)
        mid = base.copy()
        first = base.copy()
        first[:64] = -BIG
        lastv = base.copy()
        lastv[64:] = -BIG
        dd_comp[p, 0], dd_comp[p, 1], dd_comp[p, 2] = mid, first, lastv
    dw = np.zeros((128, 384), np.float32)
    dprev = 128 + q - k
    dw[:, 0:128] = np.where(dprev <= 128, -dprev, -BIG)
    dw[:, 128:256] = -np.abs(k - q)
    dnext = 128 + k - q
    dw[:, 256:384] = np.where(dnext <= 128, -dnext, -BIG)
    qs = np.ones((128, NQK), np.float32)
    for j in range(4):
        for p_ in range(128):
            qs[p_, j] = 1.0 / (8.0 * SL_DIL[2 * j + p_ // 64])
            qs[p_, 8 + j] = 1.0 / (8.0 * SL_WIN[2 * j + p_ // 64])
    ddt = np.zeros((9, 128, 512), np.float32)
    for p in range(3):
        for v in range(3):
            ddt[p * 3 + v] = np.concatenate([dd_own[p], dd_comp[p, v], dd_own[p], dd_comp[p, v]], axis=1)
    return dict(ident=ident.astype(bf), identf=ident, ones=ones.astype(bf), ddt=ddt.astype(bf),
                dw=dw.astype(bf), qscale=qs)


def _prep_shared(g_mix, w_in, g_out_dil, g_out_win, sink, w_out, g_ffn, w_gate, w_up, w_down, g_final):
    f = np.float32
    w_in = np.asarray(w_in, f)
    ext = np.concatenate([w_in[:, :, 0:512], w_in[:, :, 512:1024], w_in[:, :, 1536:2048], w_in[:, :, 2048:2176],
                          w_in[:, :, 2112:2176], w_in[:, :, 2048:2112], w_in[:, :, 1024:1536],
                          w_in[:, :, 2176:2304]], axis=2)
    sh = dict(
        w_in=np.stack([_chunked(ext[l], 8) for l in range(DEPTH)]),
        w_out=np.stack([_chunked(np.asarray(w_out[l], f), 8) for l in range(DEPTH)]),
        w_gate=np.stack([_chunked(np.asarray(w_gate[l], f), 8) for l in range(DEPTH)]),
        w_up=np.stack([_chunked(np.asarray(w_up[l], f), 8) for l in range(DEPTH)]),
        w_down=np.stack([_chunked(np.asarray(w_down[l], f), NF) for l in range(DEPTH)]),
        g_mix=np.stack([_vecT(np.asarray(g_mix[l], f)) for l in range(DEPTH)]),
        g_ffn=np.stack([_vecT(np.asarray(g_ffn[l], f)) for l in range(DEPTH)]),
        g_fin=_vecT(np.asarray(g_final, f)),
        g_finb=np.ascontiguousarray(np.broadcast_to(np.asarray(g_final, f)[None, :], (128, D))),
        g_out=np.stack([np.ascontiguousarray(np.broadcast_to(
            np.concatenate([np.asarray(g_out_dil[l], f), np.asarray(g_out_win[l], f)])[None, :], (128, D)))
            for l in range(DEPTH)]),
        sinkb=np.stack([np.ascontiguousarray(np.broadcast_to(np.asarray(sink[l], f)[None, :], (128, 8)))
                        for l in range(DEPTH)]),
    )
    sh.update(_const_tables())
    return sh


_NC_CACHE = {}


def kernel(x, g_mix, w_in, g_out_dil, g_out_win, sink, w_out, g_ffn, w_gate, w_up, w_down, g_final):
    x = np.asarray(x, np.float32)
    sh = _prep_shared(g_mix, w_in, g_out_dil, g_out_win, sink, w_out, g_ffn, w_gate, w_up, w_down, g_final)
    if "nc" not in _NC_CACHE:
        _NC_CACHE["nc"] = build_nc()
    nc = _NC_CACHE["nc"]
    in_maps = []
    for b in range(NCORES):
        m = dict(sh)
        m["xT"] = np.ascontiguousarray(x[b].T)
        in_maps.append(m)
    res = run_bass_kernel_spmd(nc, in_maps, core_ids=list(range(NCORES)))
    return np.stack([np.asarray(r["out"], np.float32) for r in res.results], axis=0)
```
